# Optimizing a Trainium2 kernel written in Bass

```python
import jax, jax.numpy as jnp
from jax import lax
import numpy as np

D_MODEL = 1024
BATCH = 4
SEQ = 4096
DEPTH = 2

CHUNK = 64
N_META = 16
Q_BLOCK = 128
D_MIX = D_MODEL
FOX_HEADS = 8
FOX_HEAD_DIM = 64
FOX_WIDTH = FOX_HEADS * FOX_HEAD_DIM
HG_HEADS = 4
HG_EXPAND = 128
HG_HEAD_V = (D_MIX - FOX_WIDTH) // HG_HEADS
HG_K = HG_HEADS * HG_EXPAND
HG_V = HG_HEADS * HG_HEAD_V
D_FF = -(-8 * D_MODEL // (3 * 256)) * 256
EPS = 1e-6
MASK_VALUE = -1e30
LOG_F_MIN = -30.0
IN_SIZES = (FOX_WIDTH, FOX_WIDTH, FOX_WIDTH, FOX_HEADS, HG_K, HG_K, HG_V, HG_V)
IN_COLS = sum(IN_SIZES)

kernel_name = "hymba_fox_hgrn2_hybrid_trunk"


def rms_norm(x, w):
    xf = x.astype(jnp.float32)
    y = xf * lax.rsqrt(jnp.mean(xf * xf, axis=-1, keepdims=True) + EPS)
    return (y * w.astype(jnp.float32)).astype(x.dtype)


def forgetting_attention(q, k, v, log_f):
    B, L, H, Dh = q.shape
    n_blk = -(-L // Q_BLOCK)
    Lp = n_blk * Q_BLOCK
    pad = Lp - L
    padf = lambda a: jnp.pad(a, [(0, 0), (0, pad)] + [(0, 0)] * (a.ndim - 2))
    q, k, v, log_f = padf(q), padf(k), padf(v), padf(log_f)
    c = jnp.cumsum(log_f, axis=1)
    cT = c.transpose(0, 2, 1)
    scale = Dh ** -0.5
    key_pos = jnp.arange(Lp)
    qb = q.reshape(B, n_blk, Q_BLOCK, H, Dh).transpose(1, 0, 2, 3, 4)
    cb = c.reshape(B, n_blk, Q_BLOCK, H).transpose(1, 0, 3, 2)

    def block(args):
        i, q_i, c_i = args
        s = jnp.einsum('bqhd,bkhd->bhqk', q_i, k, preferred_element_type=jnp.float32) * scale
        s = s + (c_i[..., :, None] - cT[:, :, None, :])
        q_pos = i * Q_BLOCK + jnp.arange(Q_BLOCK)
        s = jnp.where(key_pos[None, :] <= q_pos[:, None], s, MASK_VALUE)
        p = jax.nn.softmax(s, axis=-1)
        return jnp.einsum('bhqk,bkhd->bqhd', p.astype(v.dtype), v)

    o = lax.map(block, (jnp.arange(n_blk), qb, cb))
    return o.transpose(1, 0, 2, 3, 4).reshape(B, Lp, H, Dh)[:, :L]


def hgrn2_recurrence(q, k, v, log_f):
    B, L, H, K = q.shape
    V = v.shape[-1]
    pad = (-L) % CHUNK
    padf = lambda a: jnp.pad(a, ((0, 0), (pad, 0), (0, 0), (0, 0)))
    q, k, v, log_f = padf(q), padf(k), padf(v), padf(log_f)
    Lp = L + pad
    n = Lp // CHUNK
    to_chunks = lambda a: a.reshape(B, n, CHUNK, H, a.shape[-1]).transpose(1, 0, 3, 2, 4)
    causal = jnp.tril(jnp.ones((CHUNK, CHUNK), dtype=bool))[:, :, None]

    def step(S, inp):
        q_c, k_c, v_c, g_c = inp
        b = jnp.cumsum(g_c, axis=2)
        o_inter = jnp.einsum('bhtk,bhkv->bhtv', q_c * jnp.exp(b), S)
        rel = b[:, :, :, None, :] - b[:, :, None, :, :]
        decay = jnp.where(causal, jnp.exp(jnp.where(causal, rel, 0.0)), 0.0)
        A = jnp.einsum('bhtk,bhsk,bhtsk->bhts', q_c, k_c, decay)
        o_intra = jnp.einsum('bhts,bhsv->bhtv', A, v_c)
        b_last = b[:, :, -1]
        k_dec = k_c * jnp.exp(b_last[:, :, None, :] - b)
        S = jnp.exp(b_last)[..., None] * S + jnp.einsum('bhsk,bhsv->bhkv', k_dec, v_c)
        return S, o_inter + o_intra

    S0 = jnp.zeros((B, H, K, V), jnp.float32)
    _, o = lax.scan(step, S0, (to_chunks(q), to_chunks(k), to_chunks(v), to_chunks(log_f)))
    o = o.transpose(1, 0, 3, 2, 4).reshape(B, Lp, H, V)
    return o[:, pad:]


def setup_inputs(seed: int = 0) -> dict:
    key = jax.random.key(seed)
    ks = jax.random.split(key, 14)
    nrm = lambda k, shape, s: jax.random.normal(k, shape, jnp.float32) * s
    return {
        "x": nrm(ks[0], (BATCH, SEQ, D_MODEL), 1.0),
        "meta": nrm(ks[1], (N_META, D_MODEL), 1.0),
        "norm_mix_w": 1.0 + nrm(ks[2], (DEPTH, D_MODEL), 0.02),
        "w_in": nrm(ks[3], (DEPTH, D_MODEL, IN_COLS), D_MODEL ** -0.5),
        "fox_f_bias": 2.0 + nrm(ks[4], (DEPTH, FOX_HEADS), 0.5),
        "hgrn_lb_raw": nrm(ks[5], (DEPTH, HG_K), 1.0),
        "hgrn_norm_w": 1.0 + nrm(ks[6], (DEPTH, HG_HEAD_V), 0.02),
        "w_out": nrm(ks[7], (DEPTH, D_MIX, D_MODEL), D_MIX ** -0.5),
        "norm_ffn_w": 1.0 + nrm(ks[8], (DEPTH, D_MODEL), 0.02),
        "w_ffn_gate": nrm(ks[9], (DEPTH, D_MODEL, D_FF), D_MODEL ** -0.5),
        "w_ffn_up": nrm(ks[10], (DEPTH, D_MODEL, D_FF), D_MODEL ** -0.5),
        "w_ffn_down": nrm(ks[11], (DEPTH, D_FF, D_MODEL), D_FF ** -0.5),
        "norm_final_w": 1.0 + nrm(ks[12], (D_MODEL,), 0.02),
    }


def reference(x, meta, norm_mix_w, w_in, fox_f_bias, hgrn_lb_raw, hgrn_norm_w, w_out,
              norm_ffn_w, w_ffn_gate, w_ffn_up, w_ffn_down, norm_final_w):
    B = x.shape[0]
    h = jnp.concatenate([jnp.broadcast_to(meta[None].astype(x.dtype), (B, N_META, D_MODEL)), x], axis=1)
    L = h.shape[1]
    s_lb = jax.nn.softmax(hgrn_lb_raw.astype(jnp.float32), axis=0)
    lower_bounds = jnp.cumsum(s_lb, axis=0) - s_lb[0]
    split_at = tuple(int(v) for v in np.cumsum(IN_SIZES)[:-1])

    for l in range(DEPTH):
        u = rms_norm(h, norm_mix_w[l])
        proj = u @ w_in[l]
        fq, fk, fv, ff, hq, hf, hi, hg = jnp.split(proj, split_at, axis=-1)

        fox_log_f = jax.nn.log_sigmoid(ff.astype(jnp.float32) + fox_f_bias[l].astype(jnp.float32))
        hs = (B, L, FOX_HEADS, FOX_HEAD_DIM)
        fox_out = forgetting_attention(fq.reshape(hs), fk.reshape(hs), fv.reshape(hs), fox_log_f)
        fox_out = fox_out.reshape(B, L, FOX_WIDTH).astype(h.dtype)

        lb = lower_bounds[l]
        one_minus_f = (1.0 - lb) * jax.nn.sigmoid(-hf.astype(jnp.float32))
        log_f = jnp.maximum(jnp.log1p(-one_minus_f), LOG_F_MIN)
        k_in = -jnp.expm1(log_f)
        q_h = jax.nn.silu(hq.astype(jnp.float32)) * (HG_EXPAND ** -0.5)
        ks_ = (B, L, HG_HEADS, HG_EXPAND)
        o_h = hgrn2_recurrence(q_h.reshape(ks_), k_in.reshape(ks_),
                               hi.astype(jnp.float32).reshape(B, L, HG_HEADS, HG_HEAD_V),
                               log_f.reshape(ks_))
        o_h = rms_norm(o_h, hgrn_norm_w[l])
        hgrn_out = (o_h.reshape(B, L, HG_V) * jax.nn.silu(hg.astype(jnp.float32))).astype(h.dtype)

        mixed = jnp.concatenate([fox_out, hgrn_out], axis=-1) @ w_out[l]
        h = h + mixed

        u = rms_norm(h, norm_ffn_w[l])
        h = h + (jax.nn.silu(u @ w_ffn_gate[l]) * (u @ w_ffn_up[l])) @ w_ffn_down[l]

    return rms_norm(h, norm_final_w)[:, N_META:]
```

```python
import contextlib
import os
import numpy as np
import concourse.bass as bass
import concourse.mybir as mybir
from concourse.bass_utils import run_bass_kernel_spmd

F32 = mybir.dt.float32
BF16 = mybir.dt.bfloat16
ALU = mybir.AluOpType
AF = mybir.ActivationFunctionType

D = 1024
SEQ = 4096
NMETA = 16
LP = 4224
NT = LP // 128
NCH = LP // 64
INC = 3592
DFF = 2816
NF = DFF // 128
EPS = 1e-6
SUPER = [(i * 512, 512) for i in range(8)] + [(4096, 128)]
SUPER_E = [(i * 256, 256) for i in range(16)] + [(4096, 128)]
ARENA_W = 45000
CARENA_W = 3600
NCONST = 52

ENGS = ("pe", "act", "dve", "pool", "sp")


class Buf:
    __slots__ = ("name", "writer", "readers", "dma_cnt", "dma_id", "dma_ops")

    def __init__(self, name=""):
        self.name = name
        self.writer = None
        self.readers = []
        self.dma_cnt = 0
        self.dma_id = None
        self.dma_ops = []


class Op:
    __slots__ = ("eng", "fn", "waits", "flag", "val", "dma_buf", "is_load", "sp_idx", "dsp")

    def __init__(self, eng, fn):
        self.eng = eng
        self.fn = fn
        self.waits = []
        self.flag = False
        self.val = None
        self.dma_buf = None
        self.is_load = False
        self.sp_idx = -1
        self.dsp = -1


class Sched:
    def __init__(self, nc):
        self.nc = nc
        self.streams = {e: [] for e in ENGS}
        self.dma_bufs = []
        self.all_ops = []

    def _deps(self, op, reads, writes):
        toks = []
        for b in reads:
            if b.writer is not None:
                toks.append(b.writer)
        for b in writes:
            if b.writer is not None:
                toks.append(b.writer)
            toks.extend(b.readers)
        for t in toks:
            if t[0] == "op":
                p = t[1]
                if p is op:
                    continue
                if p.eng == op.eng and op.eng in ("pe", "sp"):
                    continue
                p.flag = True
            else:
                t = ("dma", t[1], t[1].dma_cnt)
            op.waits.append(t)

    def op(self, eng, fn, reads=(), writes=()):
        o = Op(eng, fn)
        self._deps(o, reads, writes)
        tok = ("op", o)
        for b in reads:
            b.readers = [t for t in b.readers if not (t[0] == "op" and t[1].eng == eng)]
            b.readers.append(tok)
        for b in writes:
            b.writer = tok
            b.readers = []
        self.streams[eng].append(o)
        self.all_ops.append(o)
        return o

    def dma(self, eng, out_ap, in_ap, sbuf, reads=(), writes=()):
        eng = "sp"

        def fn(e):
            return e.dma_start(out=out_ap, in_=in_ap)
        o = Op(eng, fn)
        o.is_load = len(writes) > 0
        self._deps(o, reads, writes)
        if sbuf.dma_id is None:
            sbuf.dma_id = len(self.dma_bufs)
            self.dma_bufs.append(sbuf)
        sbuf.dma_cnt += 1
        sbuf.dma_ops.append(o)
        o.dma_buf = sbuf
        tok = ("dma", sbuf, sbuf.dma_cnt)
        for b in reads:
            b.readers.append(tok)
        for b in writes:
            b.writer = tok
            b.readers = []
        self.streams[eng].append(o)
        self.all_ops.append(o)
        return o

    def _reorder_sp(self):
        sp = self.streams["sp"]
        for i, o in enumerate(sp):
            o.sp_idx = i
        last = {e: -1 for e in ENGS}
        for o in self.all_ops:
            v = last[o.eng] if o.eng != "sp" else -1
            for t in o.waits:
                if t[0] == "op":
                    v = max(v, t[1].dsp)
                else:
                    for q in t[1].dma_ops[:t[2]][-3:]:
                        v = max(v, q.sp_idx, q.dsp)
                    if t[2] > 3:
                        v = max(v, t[1].dma_ops[t[2] - 4].sp_idx)
            o.dsp = v
            if o.eng != "sp":
                last[o.eng] = max(last[o.eng], v)
        keyed = []
        prev_load_key = -1.0
        floor = -1.0
        for i, o in enumerate(sp):
            if o.fn is None:
                floor = float(i)
            if o.dma_buf is not None and o.is_load:
                k = max(float(o.dsp) + 0.5, prev_load_key, floor + 0.5)
                k = min(k, float(i))
                prev_load_key = k
                keyed.append((k, i, o))
            else:
                keyed.append((float(i), i, o))
        keyed.sort(key=lambda x: (x[0], x[1]))
        self.streams["sp"] = [x[2] for x in keyed]

    def barrier(self):
        lasts = {}
        for e in ENGS:
            for o in reversed(self.streams[e]):
                if o.dma_buf is None and o.fn is not None:
                    lasts[e] = o
                    break
        for e in ENGS:
            o = Op(e, None)
            for e2, p in lasts.items():
                if e2 != e:
                    p.flag = True
                    o.waits.append(("op", p))
            for b in self.dma_bufs:
                if b.dma_cnt:
                    o.waits.append(("dma", b, b.dma_cnt))
            self.streams[e].append(o)
            self.all_ops.append(o)

    def emit(self):
        nc = self.nc
        self._reorder_sp()
        with contextlib.ExitStack() as es:
            esem = {e: es.enter_context(nc.semaphore("s_" + e)) for e in ENGS}
            dsem = [es.enter_context(nc.semaphore("d%d" % i)) for i in range(len(self.dma_bufs))]
            for e in ENGS:
                c = 0
                for o in self.streams[e]:
                    if o.flag:
                        c += 1
                        o.val = c
            block = es.enter_context(nc.Block())

            def run(e, eng):
                seen = {}
                for o in self.streams[e]:
                    need = {}
                    for t in o.waits:
                        if t[0] == "op":
                            key, val, sem = ("e", t[1].eng), t[1].val, esem[t[1].eng]
                        else:
                            key, val, sem = ("d", t[1].dma_id), 16 * t[2], dsem[t[1].dma_id]
                        if key not in need or need[key][0] < val:
                            need[key] = (val, sem)
                    for key, (val, sem) in need.items():
                        if seen.get(key, 0) >= val:
                            continue
                        seen[key] = val
                        eng.wait_ge(sem, val)
                    if o.fn is None:
                        continue
                    ins = o.fn(eng)
                    if o.dma_buf is not None:
                        ins.then_inc(dsem[o.dma_buf.dma_id], 16)
                    elif o.flag:
                        ins.then_inc(esem[e], 1)

            @block.tensor
            def _(eng):
                run("pe", eng)

            @block.scalar
            def _(eng):
                run("act", eng)

            @block.vector
            def _(eng):
                run("dve", eng)

            @block.gpsimd
            def _(eng):
                run("pool", eng)

            @block.sync
            def _(eng):
                run("sp", eng)


class Arena:
    def __init__(self, ap_f32, width):
        self.ap = ap_f32
        self.width = width
        self.limit = width
        self.off = 0

    def reset(self, off=0):
        self.off = off

    def alloc(self, nelem, dtype=F32):
        w = (nelem + 1) // 2 if dtype == BF16 else nelem
        w = (w + 7) // 8 * 8
        assert self.off + w <= self.limit, ("arena overflow", self.off, w, self.limit)
        a = self.ap[:, self.off:self.off + w]
        self.off += w
        if dtype == BF16:
            a = a.bitcast(BF16)
        return a[:, 0:nelem]


def build_program(debug=False, upto=None):
    nc = bass.Bass("TRN2", target_bir_lowering=False)
    dt = nc.dram_tensor
    xin = dt("xin", [LP, D], F32, kind="ExternalInput").ap()
    w_in = dt("w_in", [2, D, INC], F32, kind="ExternalInput").ap()
    w_out = dt("w_out", [2, D, D], F32, kind="ExternalInput").ap()
    w_g = dt("w_g", [2, D, DFF], F32, kind="ExternalInput").ap()
    w_u = dt("w_u", [2, D, DFF], F32, kind="ExternalInput").ap()
    w_d = dt("w_d", [2, DFF, D], F32, kind="ExternalInput").ap()
    cst_d = dt("consts", [128, NCONST], F32, kind="ExternalInput").ap()
    out = dt("out", [SEQ, D], F32, kind="ExternalOutput").ap()
    kd = "ExternalOutput" if debug else "Internal"
    hT = dt("hT", [D, LP], F32, kind=kd).ap()
    Qd = dt("Qd", [8, 70, LP], BF16, kind=kd).ap()
    Kd = dt("Kd", [8, 70, LP], BF16, kind=kd).ap()
    Vd = dt("Vd", [8, 128, NT * 65], BF16, kind=kd).ap()
    QEd = dt("QEd", [4, 128, LP], BF16, kind=kd).ap()
    KEd = dt("KEd", [4, 128, LP], BF16, kind=kd).ap()
    Gd = dt("Gd", [4, 128, LP], BF16, kind=kd).ap()
    ELd = dt("ELd", [4, 128, NCH], F32, kind=kd).ap()
    VHd = dt("VHd", [NCH, 64, 512], BF16, kind=kd).ap()
    Cfd = dt("Cfd", [4, 128, LP], BF16, kind=kd).ap()
    Chd = dt("Chd", [4, 128, LP], BF16, kind=kd).ap()
    hT_v = hT.rearrange("(c p) t -> p c t", p=128)

    es = contextlib.ExitStack()
    with es:
        arena_t = es.enter_context(nc.sbuf_tensor("arena", [128, ARENA_W], F32))
        carena_t = es.enter_context(nc.sbuf_tensor("carena", [128, CARENA_W], F32))
        banks = [es.enter_context(nc.psum_tensor("bank%d" % i, [128, 512], F32)) for i in range(8)]
        pa = [b.ap() for b in banks]
        pab = [b.ap().bitcast(BF16) for b in banks]
        pb = [Buf("pb%d" % i) for i in range(8)]
        A = Arena(arena_t.ap(), ARENA_W)
        WF_W = 2 * (8 * DFF // 2)
        WF_OFF = ARENA_W - WF_W
        ffn_w = {}
        CA = Arena(carena_t.ap(), CARENA_W)
        S = Sched(nc)
        bankctr = [0]

        def nextbank(lo=0, hi=8):
            b = lo + bankctr[0] % (hi - lo)
            bankctr[0] += 1
            return b

        def MM(o, lhsT, rhs, start, stop, reads, writes):
            S.op("pe", lambda e: e.matmul(o, lhsT=lhsT, rhs=rhs, start=start, stop=stop), reads, writes)

        def TR(o, in_, ident, reads, writes):
            S.op("pe", lambda e: e.transpose(out=o, in_=in_, identity=ident), reads, writes)

        def ACTV(o, in_, func, reads, writes, bias=0.0, scale=1.0):
            S.op("act", lambda e: e.activation(out=o, in_=in_, func=func, bias=bias, scale=scale), reads, writes)

        def COPY(eng, o, in_, reads, writes):
            if eng == "act":
                S.op("act", lambda e: e.copy(out=o, in_=in_), reads, writes)
            else:
                S.op(eng, lambda e: e.tensor_copy(out=o, in_=in_), reads, writes)

        def SCALE(eng, o, in_, sc, reads, writes):
            if eng == "act":
                S.op("act", lambda e: e.activation(out=o, in_=in_, func=AF.Copy, scale=sc), reads, writes)
            else:
                S.op(eng, lambda e: e.tensor_scalar(out=o, in0=in_, scalar1=sc, scalar2=None, op0=ALU.mult), reads, writes)

        def TT(eng, o, in0, in1, op, reads, writes):
            S.op(eng, lambda e: e.tensor_tensor(out=o, in0=in0, in1=in1, op=op), reads, writes)

        def TS(eng, o, in0, s1, s2, op0, op1, reads, writes):
            if s2 is None:
                S.op(eng, lambda e: e.tensor_scalar(out=o, in0=in0, scalar1=s1, scalar2=None, op0=op0), reads, writes)
            else:
                S.op(eng, lambda e: e.tensor_scalar(out=o, in0=in0, scalar1=s1, scalar2=s2, op0=op0, op1=op1), reads, writes)

        def STT(eng, o, in0, scalar, in1, op0, op1, reads, writes):
            S.op(eng, lambda e: e.scalar_tensor_tensor(out=o, in0=in0, scalar=scalar, in1=in1, op0=op0, op1=op1), reads, writes)

        def MEMSET(eng, o, val, writes):
            S.op(eng, lambda e: e.memset(o, val), (), writes)

        def RECIP(o, in_, reads, writes):
            S.op("dve", lambda e: e.reciprocal(out=o, in_=in_), reads, writes)

        bC = Buf("consts")
        ident32 = CA.alloc(128)
        identb = CA.alloc(128, BF16)
        onesb = CA.alloc(128, BF16)
        onesf = CA.alloc(128)
        maskA = CA.alloc(64)
        cst = CA.alloc(NCONST)
        oml = CA.alloc(8)
        negb = CA.alloc(2)
        ones8 = CA.alloc(512)
        segm = CA.alloc(512)
        ones3 = CA.alloc(3 * 512, BF16)
        tmpc = CA.alloc(4)
        ELp = CA.alloc(3 * 4 * NCH)
        ELp4 = ELp.rearrange("p (k h c) -> p k h c", h=4, c=NCH)
        bEL = Buf("EL")
        MEMSET("pool", ident32, 0.0, [bC])
        S.op("pool", lambda e: e.affine_select(out=ident32, in_=ident32, pattern=[[1, 128]], compare_op=ALU.not_equal,
                                              fill=1.0, base=0, channel_multiplier=-1), [bC], [bC])
        COPY("pool", identb, ident32, [bC], [bC])
        MEMSET("pool", onesf, 1.0, [bC])
        COPY("pool", onesb, onesf, [bC], [bC])
        S.op("pool", lambda e: e.affine_select(out=maskA[0:64, :], in_=onesf[0:64, 0:64], pattern=[[1, 64]],
                                              compare_op=ALU.is_ge, fill=0.0, base=0, channel_multiplier=-1), [bC], [bC])
        S.dma("sp", cst, cst_d, bC, writes=[bC])
        MEMSET("pool", oml, 1.0, [bC])
        TT("dve", tmpc, cst[:, 40:44], cst[:, 44:48], ALU.subtract, [bC], [bC])
        ACTV(oml[:, 4:8], tmpc, AF.Sigmoid, [bC], [bC])
        TS("dve", negb[0:8, :], cst[0:8, 50:52], -1.0, None, ALU.mult, None, [bC], [bC])
        omlh = CA.alloc(8)
        nomlh = CA.alloc(8)
        TS("dve", omlh, oml, 0.5, None, ALU.mult, None, [bC], [bC])
        TS("dve", nomlh, oml, -0.5, None, ALU.mult, None, [bC], [bC])
        MEMSET("pool", ones8, 1.0, [bC])
        MEMSET("pool", segm, 1.0, [bC])
        MEMSET("pool", segm.rearrange("p (c t) -> p c t", t=64)[:, :, 0:1], 0.0, [bC])
        MEMSET("pool", ones3, 1.0, [bC])

        def rms_feat(hs3, b_hs, n, sq3, b_sq, rs, b_rs, nfeat):
            nchunk = hs3.shape[1]
            ACTV(sq3[:, :, 0:n], hs3[:, :, 0:n], AF.Square, [b_hs], [b_sq])
            bk = nextbank()
            for c in range(nchunk):
                MM(pa[bk][:, 0:n], onesb, sq3[:, c, 0:n], c == 0, c == nchunk - 1, [b_sq, bC], [pb[bk]])
            TS("dve", rs[:, 0:n], pa[bk][:, 0:n], 1.0 / nfeat, EPS, ALU.mult, ALU.add, [pb[bk]], [b_rs])
            ACTV(rs[:, 0:n], rs[:, 0:n], AF.Ln, [b_rs], [b_rs])
            ACTV(rs[:, 0:n], rs[:, 0:n], AF.Exp, [b_rs], [b_rs], scale=-0.5)

        def pass0():
            A.reset()
            A.limit = ARENA_W
            xt = [A.alloc(D) for _ in range(2)]
            bxt = [Buf(), Buf()]
            ht = [A.alloc(8 * 512) for _ in range(2)]
            bht = [Buf(), Buf()]
            k = 0
            for si, (t0, n) in enumerate(SUPER):
                h3 = ht[si % 2].rearrange("p (c t) -> p c t", t=512)
                bh = bht[si % 2]
                for jj in range(n // 128):
                    j = t0 // 128 + jj
                    x_, bx = xt[k % 2], bxt[k % 2]
                    k += 1
                    S.dma("sp", x_, xin[j * 128:(j + 1) * 128, :], bx, writes=[bx])
                    for half in range(2):
                        bk = nextbank()
                        for q in range(4):
                            c = half * 4 + q
                            TR(pa[bk][:, q * 128:(q + 1) * 128], x_[:, c * 128:(c + 1) * 128], ident32,
                               [bx, bC], [pb[bk]])
                        COPY("act" if half == 0 else "dve", h3[:, half * 4:half * 4 + 4, jj * 128:(jj + 1) * 128],
                             pa[bk].rearrange("p (q t) -> p q t", t=128), [pb[bk]], [bh])
                S.dma("pool", hT_v[:, :, t0:t0 + n], h3[:, :, 0:n], bh, reads=[bh])
            S.barrier()

        def passA(l):
            A.reset()
            A.limit = ARENA_W
            Win = A.alloc(8 * INC, BF16)
            Win3 = Win.rearrange("p (c n) -> p c n", n=INC)
            bW = [Buf() for _ in range(8)]
            mark = A.off
            stg = [A.alloc(INC) for _ in range(2)]
            bstg = [Buf(), Buf()]
            for c in range(8):
                s_, bs = stg[c % 2], bstg[c % 2]
                S.dma("sp", s_, w_in[l, c * 128:(c + 1) * 128, :], bs, writes=[bs])
                half = INC // 2
                SCALE("act", Win3[:, c, 0:half], s_[:, 0:half], cst[:, l * 8 + c:l * 8 + c + 1], [bs, bC], [bW[c]])
                TS("dve", Win3[:, c, half:INC], s_[:, half:INC], cst[:, l * 8 + c:l * 8 + c + 1], None, ALU.mult, None,
                   [bs, bC], [bW[c]])
            S.barrier()
            A.reset(mark)
            hs = A.alloc(8 * 512); b_hs = Buf()
            hs3 = hs.rearrange("p (c t) -> p c t", t=512)
            sq = A.alloc(8 * 512, BF16); b_sq = Buf()
            sq3 = sq.rearrange("p (c t) -> p c t", t=512)
            rs = A.alloc(512); b_rs = Buf()
            uT = [A.alloc(8 * 512, BF16) for _ in range(2)]
            b_uT = [Buf(), Buf()]
            Qst = A.alloc(4 * 512, BF16); b_Q = Buf()
            Kst = A.alloc(4 * 512, BF16); b_K = Buf()
            Q3 = Qst.rearrange("p (h t) -> p h t", t=512)
            K3 = Kst.rearrange("p (h t) -> p h t", t=512)
            Vst = A.alloc(4 * 8 * 65, BF16); b_V = Buf()
            V4 = Vst.rearrange("p (h j d) -> p h j d", j=4, d=65)
            V3m = Vst.rearrange("p (h m) -> p h m", m=4 * 65)
            MEMSET("pool", Vst, 1.0, [b_V])
            t1 = A.alloc(512); t2 = A.alloc(512); cc = A.alloc(512); r1 = A.alloc(512)
            b_f = Buf()
            cs3 = A.alloc(6 * 512, BF16); b_cs = Buf()
            cs3v = cs3.rearrange("p (r t) -> p r t", t=512)
            carry = A.alloc(1); b_carry = Buf()
            MEMSET("dve", carry, 0.0, [b_carry])
            kk = A.alloc(512); lf = A.alloc(512); bb = A.alloc(512)
            eb = A.alloc(512); enb = A.alloc(512)
            th = [A.alloc(512) for _ in range(4)]; b_th = [Buf() for _ in range(4)]
            sqhs = [A.alloc(512) for _ in range(4)]; b_sqhs = [Buf() for _ in range(4)]
            b_h = [Buf() for _ in range(7)]
            qe = A.alloc(4 * 512, BF16); ke = A.alloc(4 * 512, BF16); gg = A.alloc(4 * 512, BF16)
            b_qe, b_ke, b_gg = Buf(), Buf(), Buf()
            qe3 = qe.rearrange("p (h t) -> p h t", t=512)
            ke3 = ke.rearrange("p (h t) -> p h t", t=512)
            gg3 = gg.rearrange("p (h t) -> p h t", t=512)
            VH = A.alloc(4 * 512, BF16); b_VH = Buf()
            VH3 = VH.rearrange("p (c n) -> p c n", n=512)
            allW = bW

            def norm(si):
                t0, n = SUPER[si]
                u3n = uT[si % 2].rearrange("p (c t) -> p c t", t=512)
                S.dma("sp", hs3[:, :, 0:n], hT_v[:, :, t0:t0 + n], b_hs, writes=[b_hs])
                rms_feat(hs3, b_hs, n, sq3, b_sq, rs, b_rs, D)
                for c in range(8):
                    TT("dve", u3n[:, c, 0:n], hs3[:, c, 0:n], rs[:, 0:n], ALU.mult,
                       [b_hs, b_rs], [b_uT[si % 2]])

            norm(0)
            ypend = []
            for si, (t0, n) in enumerate(SUPER):
                u_, bu = uT[si % 2], b_uT[si % 2]
                u3 = u_.rearrange("p (c t) -> p c t", t=512)
                rd = [bu] + allW

                def proj_fm(c0, m, bk):
                    for c in range(8):
                        MM(pa[bk][0:m, 0:n], Win3[:, c, c0:c0 + m], u3[:, c, 0:n], c == 0, c == 7, rd, [pb[bk]])

                bk = nextbank()
                proj_fm(1536, 8, bk)
                ACTV(t1[0:8, 0:n], pa[bk][0:8, 0:n], AF.Exp, [pb[bk], bC], [b_f], bias=negb[0:8, l:l + 1], scale=-1.0)
                ACTV(t2[0:8, 0:n], t1[0:8, 0:n], AF.Ln, [b_f], [b_f], bias=1.0, scale=1.0)
                S.op("dve", lambda e, n=n: e.tensor_tensor_scan(out=cc[0:8, 0:n], data0=ones8[0:8, 0:n], data1=t2[0:8, 0:n],
                                                             initial=carry[0:8, 0:1], op0=ALU.mult, op1=ALU.subtract),
                     [b_f, b_carry, bC], [b_f])
                COPY("dve", carry[0:8, 0:1], cc[0:8, n - 1:n], [b_f], [b_carry])
                COPY("dve", cs3v[0:8, 0, 0:n], cc[0:8, 0:n], [b_f], [b_cs])
                TT("dve", r1[0:8, 0:n], cc[0:8, 0:n], cs3v[0:8, 0, 0:n], ALU.subtract, [b_f, b_cs], [b_f])
                COPY("dve", cs3v[0:8, 1, 0:n], r1[0:8, 0:n], [b_f], [b_cs])
                TT("dve", r1[0:8, 0:n], r1[0:8, 0:n], cs3v[0:8, 1, 0:n], ALU.subtract, [b_f, b_cs], [b_f])
                COPY("dve", cs3v[0:8, 2, 0:n], r1[0:8, 0:n], [b_f], [b_cs])
                TS("dve", cs3v[0:8, 3:6, 0:n], cs3v[0:8, 0:3, 0:n], -1.0, None, ALU.mult, None, [b_cs], [b_cs])
                S.dma("pool", Qd[:, 64:67, t0:t0 + n], cs3v[0:8, 0:3, 0:n], b_cs, reads=[b_cs])
                S.dma("pool", Kd[:, 67:70, t0:t0 + n], cs3v[0:8, 3:6, 0:n], b_cs, reads=[b_cs])
                o3 = ones3.rearrange("p (r t) -> p r t", t=512)
                S.dma("pool", Qd[:, 67:70, t0:t0 + n], o3[0:8, :, 0:n], bC, reads=[bC])
                S.dma("pool", Kd[:, 64:67, t0:t0 + n], o3[0:8, :, 0:n], bC, reads=[bC])
                if ypend:
                    ypend.pop(0)()
                Qd_v = Qd.rearrange("(m two) p t -> two p m t", two=2)
                Kd_v = Kd.rearrange("(m two) p t -> two p m t", two=2)
                for m in range(4):
                    bk = nextbank()
                    proj_fm(m * 128, 128, bk)
                    ACTV(Q3[:, m, 0:n], pa[bk][:, 0:n], AF.Copy, [pb[bk]], [b_Q], scale=0.125)
                for two in range(2):
                    S.dma("pool", Qd_v[two, 0:64, :, t0:t0 + n], Q3[two * 64:(two + 1) * 64, 0:4, 0:n], b_Q, reads=[b_Q])
                if ypend:
                    ypend.pop(0)()
                for m in range(4):
                    bk = nextbank()
                    proj_fm(512 + m * 128, 128, bk)
                    COPY("dve", K3[:, m, 0:n], pa[bk][:, 0:n], [pb[bk]], [b_K])
                for two in range(2):
                    S.dma("pool", Kd_v[two, 0:64, :, t0:t0 + n], K3[two * 64:(two + 1) * 64, 0:4, 0:n], b_K, reads=[b_K])
                if ypend:
                    ypend.pop(0)()
                for jj in range(n // 128):
                    bk = nextbank()
                    for c in range(8):
                        MM(pa[bk][:, 0:512], u3[:, c, jj * 128:(jj + 1) * 128], Win3[:, c, 1024:1536], c == 0, c == 7,
                           rd, [pb[bk]])
                    COPY("act" if jj % 2 == 0 else "dve", V4[:, :, jj, 0:64],
                         pa[bk].rearrange("p (h d) -> p h d", d=64), [pb[bk]], [b_V])
                j0 = t0 // 128
                nj = n // 128
                S.dma("pool", Vd[:, :, j0 * 65:(j0 + nj) * 65].rearrange("h p m -> p h m"),
                      V3m[:, :, 0:nj * 65], b_V, reads=[b_V])
                while ypend:
                    ypend.pop(0)()
                for h in range(4):
                    bk = nextbank()
                    proj_fm(1544 + 512 + h * 128, 128, bk)
                    ACTV(th[h][:, 0:n], pa[bk][:, 0:n], AF.Tanh, [pb[bk]], [b_th[h]], scale=0.5)
                    bk = nextbank()
                    proj_fm(1544 + h * 128, 128, bk)
                    ACTV(sqhs[h][:, 0:n], pa[bk][:, 0:n], AF.Silu, [pb[bk]], [b_sqhs[h]])
                    bk = nextbank()
                    proj_fm(1544 + 1536 + h * 128, 128, bk)
                    ACTV(gg3[:, h, 0:n], pa[bk][:, 0:n], AF.Silu, [pb[bk]], [b_gg])
                if si + 1 < len(SUPER):
                    norm(si + 1)
                def make_y(h, t0, n):
                    def ypiece():
                        TS("dve", kk[:, 0:n], th[h][:, 0:n], nomlh[:, l * 4 + h:l * 4 + h + 1],
                           omlh[:, l * 4 + h:l * 4 + h + 1], ALU.mult, ALU.add, [b_th[h], bC], [b_h[1]])
                        ACTV(lf[:, 0:n], kk[:, 0:n], AF.Ln, [b_h[1]], [b_h[2]], bias=1.0, scale=-1.0)
                        TS("dve", lf[:, 0:n], lf[:, 0:n], -30.0, None, ALU.max, None, [b_h[2]], [b_h[2]])
                        S.op("dve", lambda e: e.tensor_tensor_scan(out=bb[:, 0:n], data0=segm[:, 0:n], data1=lf[:, 0:n],
                                                                  initial=0.0, op0=ALU.mult, op1=ALU.add),
                             [b_h[2], bC], [b_h[3]])
                        c0_, c1_ = t0 // 64, (t0 + n) // 64
                        bb3 = bb[:, 0:n].rearrange("p (c t) -> p c t", t=64)
                        lf3 = lf[:, 0:n].rearrange("p (c t) -> p c t", t=64)
                        ACTV(ELp4[:, 0, h, c0_:c1_], bb3[:, :, 63], AF.Exp, [b_h[3]], [bEL])
                        ACTV(ELp4[:, 2, h, c0_:c1_], bb3[:, :, 31], AF.Exp, [b_h[3]], [bEL])
                        TT("dve", lf3, bb3, bb3[:, :, 31:32].broadcast_to([128, n // 64, 64]), ALU.subtract,
                           [b_h[3], b_h[2]], [b_h[2]])
                        ACTV(eb[:, 0:n], lf[:, 0:n], AF.Exp, [b_h[2]], [b_h[4]])
                        ACTV(enb[:, 0:n], lf[:, 0:n], AF.Exp, [b_h[2]], [b_h[5]], scale=-1.0)
                        TT("dve", ke3[:, h, 0:n], kk[:, 0:n], enb[:, 0:n], ALU.mult, [b_h[1], b_h[5]], [b_ke])
                        COPY("pool", ELp4[:, 1, h, c0_:c1_], eb[:, 0:n].rearrange("p (c t) -> p c t", t=64)[:, :, 63],
                             [b_h[4]], [bEL])
                        STT("dve", qe3[:, h, 0:n], sqhs[h][:, 0:n], 128.0 ** -0.5, eb[:, 0:n], ALU.mult, ALU.mult,
                            [b_sqhs[h], b_h[4]], [b_qe])
                    return ypiece

                def make_ystores(t0, n):
                    def ystores():
                        S.dma("pool", QEd[:, :, t0:t0 + n].rearrange("h p t -> p h t"), qe3[:, :, 0:n], b_qe, reads=[b_qe])
                        S.dma("pool", KEd[:, :, t0:t0 + n].rearrange("h p t -> p h t"), ke3[:, :, 0:n], b_ke, reads=[b_ke])
                    return ystores
                ypend.extend([make_y(h, t0, n) for h in range(4)] + [make_ystores(t0, n)])
                S.dma("pool", Gd[:, :, t0:t0 + n].rearrange("h p t -> p h t"), gg3[:, :, 0:n], b_gg, reads=[b_gg])
                VHd_tok = VHd.rearrange("c p n -> (c p) n")
                for jj in range(n // 128):
                    bk = nextbank()
                    for c in range(8):
                        MM(pa[bk][:, 0:512], u3[:, c, jj * 128:(jj + 1) * 128], Win3[:, c, 1544 + 1024:1544 + 1536],
                           c == 0, c == 7, rd, [pb[bk]])
                    COPY("act" if jj % 2 == 0 else "dve", VH3[:, jj, :], pa[bk][:, 0:512], [pb[bk]], [b_VH])
                S.dma("pool", VHd_tok[t0:t0 + n, :].rearrange("(j p) n -> p j n", p=128), VH3[:, 0:n // 128, :],
                      b_VH, reads=[b_VH])
            while ypend:
                ypend.pop(0)()
            S.barrier()

        def passB(l):
            A.reset()
            A.limit = WF_OFF
            AW = Arena(arena_t.ap()[:, WF_OFF:ARENA_W], WF_W)
            Wg3 = AW.alloc(8 * DFF, BF16).rearrange("p (c n) -> p c n", n=DFF)
            Wu3 = AW.alloc(8 * DFF, BF16).rearrange("p (c n) -> p c n", n=DFF)
            bWg = [Buf() for _ in range(8)]; bWu = [Buf() for _ in range(8)]
            ffn_w[l] = (Wg3, Wu3, bWg, bWu)
            PW = DFF // 4
            wstg = [A.alloc(PW) for _ in range(2)]; bwstg = [Buf(), Buf()]
            pieces = []
            for (wsrc, W3, bWl) in ((w_g, Wg3, bWg), (w_u, Wu3, bWu)):
                for c in range(8):
                    for q in range(4):
                        pieces.append((wsrc[l, c * 128:(c + 1) * 128, q * PW:(q + 1) * PW], W3[:, c, q * PW:(q + 1) * PW],
                                       cst[:, 16 + l * 8 + c:16 + l * 8 + c + 1], bWl[c]))
            pstate = [0]

            def prefetch_step():
                k = pstate[0]
                if k >= len(pieces):
                    return
                pstate[0] += 1
                src, dst, sc, bw = pieces[k]
                S.dma("sp", wstg[k % 2], src, bwstg[k % 2], writes=[bwstg[k % 2]])
                SCALE("dve", dst, wstg[k % 2], sc, [bwstg[k % 2], bC], [bw])
            Ka = [A.alloc(LP, BF16) for _ in range(2)]; bKa = [Buf(), Buf()]
            Va = [A.alloc(NT * 65, BF16) for _ in range(2)]; bVa = [Buf(), Buf()]
            Qa = [A.alloc(512, BF16) for _ in range(2)]; bQa = [Buf(), Buf()]
            NPT = 7
            pt = [A.alloc(512, BF16) for _ in range(NPT)]; bpt = [Buf() for _ in range(NPT)]
            osb = [A.alloc(512) for _ in range(2)]; b_osb = [Buf(), Buf()]
            on = [A.alloc(512, BF16) for _ in range(2)]; b_on = [Buf(), Buf()]
            rcb = A.alloc(512); b_rcb = Buf()
            qi = 0
            pi = 0
            pending = [None]

            def make_part2(h, t0, n, par):
                def part2():
                    bk = 5
                    MM(pa[bk][0:64, 0:n], onesf[64:65, 0:64], osb[par][64:65, 0:n], True, True,
                       [b_osb[par], bC], [pb[bk]])
                    RECIP(rcb[0:64, 0:n], pa[bk][0:64, 0:n], [pb[bk]], [b_rcb])
                    TT("dve", on[par][0:64, 0:n], osb[par][0:64, 0:n], rcb[0:64, 0:n], ALU.mult,
                       [b_osb[par], b_rcb], [b_on[par]])
                    S.dma("pool", Cfd[h // 2, (h % 2) * 64:(h % 2) * 64 + 64, t0:t0 + n], on[par][0:64, 0:n], b_on[par], reads=[b_on[par]])
                return part2

            for h in range(8):
                ka, bka = Ka[h % 2], bKa[h % 2]
                va, bva = Va[h % 2], bVa[h % 2]
                va3 = va.rearrange("p (j d) -> p j d", d=65)
                S.dma("sp", ka[0:70, :], Kd[h], bka, writes=[bka])
                S.dma("sp", va, Vd[h], bva, writes=[bva])
                for si, (t0, n) in enumerate(SUPER):
                    qa, bqa = Qa[qi % 2], bQa[qi % 2]
                    par = qi % 2
                    ob = 6 + qi % 2
                    qi += 1
                    S.dma("sp", qa[0:70, 0:n], Qd[h, :, t0:t0 + n], bqa, writes=[bqa])
                    prefetch_step()
                    nk = (t0 + n) // 128
                    LA = 4
                    slots = {}
                    for idx in range(nk + LA):
                        if idx < nk:
                            j = idx
                            qlo = max(0, j * 128 - t0)
                            w = n - qlo
                            sb = nextbank(0, 5)
                            p_, bp = pt[pi % NPT], bpt[pi % NPT]
                            pi += 1
                            slots[j] = (p_, bp, qlo, w)
                            MM(pa[sb][:, 0:w], ka[0:70, j * 128:(j + 1) * 128], qa[0:70, qlo:n], True, True,
                               [bka, bqa], [pb[sb]])
                            ACTV(p_[:, 0:w], pa[sb][:, 0:w], AF.Exp, [pb[sb]], [bp])
                            if j * 128 >= t0:
                                S.op("pool", lambda e, p_=p_: e.affine_select(out=p_[:, 0:128], in_=p_[:, 0:128],
                                                                              pattern=[[1, 128]], compare_op=ALU.is_ge,
                                                                              fill=0.0, base=0, channel_multiplier=-1),
                                     [bp], [bp])
                        if idx >= LA:
                            j = idx - LA
                            p_, bp, qlo, w = slots.pop(j)
                            MM(pa[ob][0:65, qlo:n], va3[:, j, :], p_[:, 0:w], j == 0, j == nk - 1, [bva, bp], [pb[ob]])
                        if idx == min(LA + 2, nk + LA - 1) and pending[0] is not None:
                            pending[0]()
                            pending[0] = None
                    COPY("act", osb[par][0:65, 0:n], pa[ob][0:65, 0:n], [pb[ob]], [b_osb[par]])
                    pending[0] = make_part2(h, t0, n, par)
            pending[0]()
            while pstate[0] < len(pieces):
                prefetch_step()
            S.barrier()

        def passC(l):
            A.reset()
            A.limit = WF_OFF
            qeb = [A.alloc(4 * 512, BF16) for _ in range(2)]; b_qe = [Buf(), Buf()]
            keb = [A.alloc(4 * 512, BF16) for _ in range(2)]; b_ke = [Buf(), Buf()]
            ggb = [A.alloc(4 * 512, BF16) for _ in range(2)]; b_gg = [Buf(), Buf()]
            vhb = [A.alloc(8 * 512, BF16) for _ in range(2)]; b_vh = [Buf(), Buf()]
            Sf = [A.alloc(128) for _ in range(4)]; bSf = [Buf() for _ in range(4)]
            Sb = [A.alloc(128, BF16) for _ in range(4)]; bSb = [Buf() for _ in range(4)]
            Sel = [A.alloc(128) for _ in range(4)]; bSel = [Buf() for _ in range(4)]
            ktk = [A.alloc(128, BF16) for _ in range(4)]; bktk = [Buf() for _ in range(4)]
            at = [A.alloc(64, BF16) for _ in range(4)]; bat = [Buf() for _ in range(4)]
            osq = [A.alloc(512, BF16) for _ in range(2)]; bosq = [Buf(), Buf()]
            rs = [A.alloc(512) for _ in range(2)]; brs = [Buf(), Buf()]
            yy = [A.alloc(512) for _ in range(2)]; byy = [Buf(), Buf()]
            co = [A.alloc(4 * 512, BF16) for _ in range(2)]; bco = [Buf(), Buf()]
            for h in range(4):
                MEMSET("pool", Sf[h], 0.0, [bSf[h]])
                MEMSET("pool", Sb[h], 0.0, [bSb[h]])
                MEMSET("pool", at[h], 0.0, [bat[h]])
            k2 = 0
            for si, (t0, n) in enumerate(SUPER):
                r = si % 2
                qe3 = qeb[r].rearrange("p (h t) -> p h t", t=512)
                ke3 = keb[r].rearrange("p (h t) -> p h t", t=512)
                gg3 = ggb[r].rearrange("p (h t) -> p h t", t=512)
                vh3 = vhb[r].rearrange("p (c n) -> p c n", n=512)
                co3 = co[r].rearrange("p (h t) -> p h t", t=512)
                ncs = n // 64
                S.dma("sp", qe3[:, :, 0:n], QEd[:, :, t0:t0 + n].rearrange("h p t -> p h t"), b_qe[r], writes=[b_qe[r]])
                S.dma("sp", ke3[:, :, 0:n], KEd[:, :, t0:t0 + n].rearrange("h p t -> p h t"), b_ke[r], writes=[b_ke[r]])
                S.dma("sp", vh3[0:64, 0:ncs, :], VHd[t0 // 64:(t0 + n) // 64, :, :].rearrange("c p n -> p c n"),
                      b_vh[r], writes=[b_vh[r]])
                S.dma("sp", gg3[:, :, 0:n], Gd[:, :, t0:t0 + n].rearrange("h p t -> p h t"), b_gg[r], writes=[b_gg[r]])
                CDBG = os.environ.get("CDBG", "")
                if CDBG and si > 0:
                    break
                CSTEP = int(os.environ.get("CSTEP", "9"))
                CNC = int(os.environ.get("CNC", "99"))
                CNH = int(os.environ.get("CNH", "4"))
                for ci in range(min(ncs, CNC) if CDBG != "loads" else 0):
                    cs = slice(ci * 64, (ci + 1) * 64)
                    for h in range(CNH):
                        tb = 4 + h
                        ob = h
                        kec = ke3[:, h, cs]
                        qec = qe3[:, h, cs]
                        vhc = vh3[0:64, ci, h * 128:(h + 1) * 128]
                        CSKIP = os.environ.get("CSKIP", "")
                        if "t" not in CSKIP:
                            TR(pab[tb][0:64, 512:640], kec, identb, [b_ke[r], bC], [pb[tb]])
                        if "m" not in CSKIP:
                            MM(pa[tb][0:32, 0:32], ke3[:, h, ci * 64:ci * 64 + 32], qe3[:, h, ci * 64:ci * 64 + 32],
                               True, True, [b_ke[r], b_qe[r]], [pb[tb]])
                            MM(pa[tb][0:64, 32:64], kec, qe3[:, h, ci * 64 + 32:ci * 64 + 64], True, True,
                               [b_ke[r], b_qe[r]], [pb[tb]])
                        if "a" not in CSKIP:
                            COPY(os.environ.get("CKT", "dve"), ktk[h][0:64, :], pab[tb][0:64, 512:640], [pb[tb]], [bktk[h]])
                        if "d" not in CSKIP:
                            TT("dve", at[h][0:32, 0:32], pa[tb][0:32, 0:32], maskA[0:32, 0:32], ALU.mult,
                               [pb[tb], bC], [bat[h]])
                            TT("dve", at[h][0:64, 32:64], pa[tb][0:64, 32:64], maskA[0:64, 32:64], ALU.mult,
                               [pb[tb], bC], [bat[h]])
                        if CSTEP < 2:
                            continue
                        MM(pa[ob][:, cs], Sb[h], qec, True, False, [bSb[h], b_qe[r]], [pb[ob]])
                        MM(pa[ob][:, cs], vhc, at[h][0:64, :], False, True, [b_vh[r], bat[h]], [pb[ob]])
                        if CSTEP < 3:
                            continue
                        MM(pa[tb][:, 128:256], ktk[h][0:64, :], vhc, True, True, [bktk[h], b_vh[r]], [pb[tb]])
                        if CSTEP < 4:
                            continue
                        cg = t0 // 64 + ci
                        el1 = ELp4[:, 0, h, cg:cg + 1]
                        el2 = ELp4[:, 1, h, cg:cg + 1]
                        SCALE("act", Sel[h], Sf[h], el1, [bSf[h], bEL], [bSel[h]])
                        STT("dve", Sf[h], pa[tb][:, 128:256], el2, Sel[h], ALU.mult, ALU.add,
                            [pb[tb], bEL, bSel[h]], [bSf[h]])
                        if cg + 1 < NCH:
                            SCALE("act", Sb[h], Sf[h], ELp4[:, 2, h, cg + 1:cg + 2], [bSf[h], bEL], [bSb[h]])
                for h in range(4 if CDBG in ("", "post") else 0):
                    ob = h
                    tb = 4 + h
                    kk_ = k2 % 2
                    k2 += 1
                    ACTV(osq[kk_][:, 0:n], pa[ob][:, 0:n], AF.Square, [pb[ob]], [bosq[kk_]])
                    MM(pa[tb][:, 0:n], onesb, osq[kk_][:, 0:n], True, True, [bosq[kk_], bC], [pb[tb]])
                    TS("dve", rs[kk_][:, 0:n], pa[tb][:, 0:n], 1.0 / 128, EPS, ALU.mult, ALU.add, [pb[tb]], [brs[kk_]])
                    ACTV(rs[kk_][:, 0:n], rs[kk_][:, 0:n], AF.Ln, [brs[kk_]], [brs[kk_]])
                    ACTV(rs[kk_][:, 0:n], rs[kk_][:, 0:n], AF.Exp, [brs[kk_]], [brs[kk_]], scale=-0.5)
                    STT("dve", yy[kk_][:, 0:n], pa[ob][:, 0:n], cst[:, 48 + l:49 + l], rs[kk_][:, 0:n], ALU.mult, ALU.mult,
                        [pb[ob], brs[kk_], bC], [byy[kk_]])
                    TT("dve", co3[:, h, 0:n], yy[kk_][:, 0:n], gg3[:, h, 0:n], ALU.mult, [byy[kk_], b_gg[r]], [bco[r]])
                if CDBG in ("", "post"):
                    S.dma("pool", Chd[:, :, t0:t0 + n].rearrange("h p t -> p h t"), co3[:, :, 0:n], bco[r], reads=[bco[r]])
            S.barrier()

        def passD(l):
            A.reset()
            A.limit = WF_OFF
            Wo = A.alloc(8 * D, BF16); Wo3 = Wo.rearrange("p (h n) -> p h n", n=D); bWo = Buf()
            stg = A.alloc(8 * D); bstg = Buf()
            stg3 = stg.rearrange("p (h n) -> p h n", n=D)
            S.dma("sp", stg3, w_out[l].rearrange("(h p) n -> p h n", p=128), bstg, writes=[bstg])
            COPY("dve", Wo3[:, 0:4, :], stg3[:, 0:4, :], [bstg], [bWo])
            COPY("act", Wo3[:, 4:8, :], stg3[:, 4:8, :], [bstg], [bWo])
            S.barrier()
            A.reset(A.off - 8 * D)
            cf = [A.alloc(4 * 512, BF16) for _ in range(2)]; bcf = [Buf(), Buf()]
            ch = [A.alloc(4 * 512, BF16) for _ in range(2)]; bch = [Buf(), Buf()]
            hs = [A.alloc(8 * 512) for _ in range(2)]; bhs = [Buf(), Buf()]
            for si, (t0, n) in enumerate(SUPER):
                r = si % 2
                cf3 = cf[r].rearrange("p (h t) -> p h t", t=512)
                ch3 = ch[r].rearrange("p (h t) -> p h t", t=512)
                hs3 = hs[r].rearrange("p (c t) -> p c t", t=512)
                S.dma("sp", cf3[:, :, 0:n], Cfd[:, :, t0:t0 + n].rearrange("h p t -> p h t"), bcf[r], writes=[bcf[r]])
                S.dma("sp", ch3[:, :, 0:n], Chd[:, :, t0:t0 + n].rearrange("h p t -> p h t"), bch[r], writes=[bch[r]])
                S.dma("sp", hs3[:, :, 0:n], hT_v[:, :, t0:t0 + n], bhs[r], writes=[bhs[r]])
                for oc in range(8):
                    bk = nextbank()
                    osl = slice(oc * 128, (oc + 1) * 128)
                    for h in range(4):
                        MM(pa[bk][:, 0:n], Wo3[:, h, osl], cf3[:, h, 0:n], h == 0, False, [bWo, bcf[r]], [pb[bk]])
                    for h in range(4):
                        MM(pa[bk][:, 0:n], Wo3[:, 4 + h, osl], ch3[:, h, 0:n], False, h == 3, [bWo, bch[r]], [pb[bk]])
                    TT("dve", hs3[:, oc, 0:n], hs3[:, oc, 0:n], pa[bk][:, 0:n], ALU.add, [bhs[r], pb[bk]], [bhs[r]])
                S.dma("pool", hT_v[:, :, t0:t0 + n], hs3[:, :, 0:n], bhs[r], reads=[bhs[r]])
            S.barrier()

        def passE(l, last):
            A.reset()
            A.limit = WF_OFF
            Wg3, Wu3, bWg, bWu = ffn_w[l]
            Wd = A.alloc(NF * D, BF16); Wd3 = Wd.rearrange("p (f n) -> p f n", n=D)
            bWd = [Buf() for _ in range(NF)]
            stg = [A.alloc(512) for _ in range(2)]; bstg = [Buf(), Buf()]
            k = 0
            for f in range(NF):
                for q in range(2):
                    s_, bs = stg[k % 2], bstg[k % 2]
                    k += 1
                    S.dma("sp", s_[:, 0:512], w_d[l, f * 128:(f + 1) * 128, q * 512:(q + 1) * 512], bs, writes=[bs])
                    COPY("act" if k % 2 == 0 else "dve", Wd3[:, f, q * 512:(q + 1) * 512], s_[:, 0:512], [bs], [bWd[f]])
            NE = 256
            hsr = [A.alloc(8 * NE) for _ in range(2)]; bhsr = [Buf(), Buf()]
            uTr = [A.alloc(8 * NE, BF16) for _ in range(2)]; buTr = [Buf(), Buf()]
            aT = A.alloc(NF * NE, BF16); baT = Buf()
            a3 = aT.rearrange("p (f t) -> p f t", t=NE)
            sq3 = a3[:, 0:8, :]
            rs = A.alloc(NE); brs = Buf()
            sg = [A.alloc(NE) for _ in range(1)]; bsg = [Buf()]
            ot = [A.alloc(512 if last else 8) for _ in range(1)]; bot = [Buf()]
            allWg, allWu, allWd = bWg, bWu, bWd
            ko = 0
            kout = [0]
            oslots = [(ot[0], bot[0]), (stg[0], bstg[0]), (stg[1], bstg[1])]
            pending_out = [None]

            def norm(si):
                t0, n = SUPER_E[si]
                hs3n = hsr[si % 2].rearrange("p (c t) -> p c t", t=NE)
                u3n = uTr[si % 2].rearrange("p (c t) -> p c t", t=NE)
                S.dma("sp", hs3n[:, :, 0:n], hT_v[:, :, t0:t0 + n], bhsr[si % 2], writes=[bhsr[si % 2]])
                rms_feat(hs3n, bhsr[si % 2], n, u3n, buTr[si % 2], rs, brs, D)
                for c in range(8):
                    TT("dve", u3n[:, c, 0:n], hs3n[:, c, 0:n], rs[:, 0:n], ALU.mult,
                       [bhsr[si % 2], brs], [buTr[si % 2]])

            norm(0)
            for si, (t0, n) in enumerate(SUPER_E):
                hs3 = hsr[si % 2].rearrange("p (c t) -> p c t", t=NE)
                bhs = bhsr[si % 2]
                u3 = uTr[si % 2].rearrange("p (c t) -> p c t", t=NE)
                buT = buTr[si % 2]
                for f in range(NF):
                    if f == 3 and pending_out[0] is not None:
                        pending_out[0]()
                        pending_out[0] = None
                    fs = slice(f * 128, (f + 1) * 128)
                    bg = nextbank()
                    for c in range(8):
                        MM(pa[bg][:, 0:n], Wg3[:, c, fs], u3[:, c, 0:n], c == 0, c == 7, [buT] + allWg, [pb[bg]])
                    bu_ = nextbank()
                    for c in range(8):
                        MM(pa[bu_][:, 0:n], Wu3[:, c, fs], u3[:, c, 0:n], c == 0, c == 7, [buT] + allWu, [pb[bu_]])
                    s_, bs = sg[0], bsg[0]
                    ACTV(s_[:, 0:n], pa[bg][:, 0:n], AF.Silu, [pb[bg]], [bs])
                    TT("dve", a3[:, f, 0:n], s_[:, 0:n], pa[bu_][:, 0:n], ALU.mult, [bs, pb[bu_]], [baT])
                if si + 1 < len(SUPER_E):
                    norm(si + 1)
                for oc in range(8):
                    bk = nextbank()
                    osl = slice(oc * 128, (oc + 1) * 128)
                    for f in range(NF):
                        MM(pa[bk][:, 0:n], Wd3[:, f, osl], a3[:, f, 0:n], f == 0, f == NF - 1, [baT] + allWd, [pb[bk]])
                    TT("dve", hs3[:, oc, 0:n], hs3[:, oc, 0:n], pa[bk][:, 0:n], ALU.add, [bhs, pb[bk]], [bhs])
                if not last:
                    S.dma("pool", hT_v[:, :, t0:t0 + n], hs3[:, :, 0:n], bhs, reads=[bhs])
                else:
                    rms_feat(hs3, bhs, n, sq3, baT, rs, brs, D)
                    for c in range(8):
                        STT("dve", hs3[:, c, 0:n], hs3[:, c, 0:n], cst[:, 32 + c:33 + c],
                            rs[:, 0:n], ALU.mult, ALU.mult, [bhs, brs, bC], [bhs])
                    def make_out(hs3, bhs, t0, n):
                        def emit_out():
                            for jj in range(n // 128):
                                j = t0 // 128 + jj
                                g0 = j * 128
                                lo = max(g0, NMETA)
                                hi = min(g0 + 128, NMETA + SEQ)
                                for half in range(2):
                                    o_, bo = oslots[kout[0] % 3]
                                    kout[0] += 1
                                    bk = nextbank()
                                    for q in range(4):
                                        c = half * 4 + q
                                        TR(pa[bk][:, q * 128:(q + 1) * 128], hs3[:, c, jj * 128:(jj + 1) * 128], ident32,
                                           [bhs, bC], [pb[bk]])
                                    COPY("act" if half == 0 else "dve", o_[:, 0:512], pa[bk][:, 0:512], [pb[bk]], [bo])
                                    if hi > lo:
                                        S.dma("pool", out[lo - NMETA:hi - NMETA, half * 512:(half + 1) * 512],
                                              o_[lo - g0:hi - g0, 0:512], bo, reads=[bo])
                        return emit_out
                    pending_out[0] = make_out(hs3, bhs, t0, n)
            if pending_out[0] is not None:
                pending_out[0]()
            S.barrier()

        plan = [("0", pass0, ())]
        for l in range(2):
            plan += [("A%d" % l, passA, (l,)), ("B%d" % l, passB, (l,)), ("C%d" % l, passC, (l,)),
                     ("D%d" % l, passD, (l,)), ("E%d" % l, passE, (l, l == 1))]
        for name, fn_, args in plan:
            fn_(*args)
            if upto is not None and name == upto:
                break
        S.emit()
    return nc


_NC_CACHE = {}


def _prep_consts(norm_mix_w, norm_ffn_w, norm_final_w, hgrn_lb_raw, hgrn_norm_w, fox_f_bias):
    c = np.zeros((128, NCONST), np.float32)
    for l in range(2):
        c[:, l * 8:(l + 1) * 8] = norm_mix_w[l].reshape(8, 128).T
        c[:, 16 + l * 8:16 + (l + 1) * 8] = norm_ffn_w[l].reshape(8, 128).T
        c[:, 40 + l * 4:40 + (l + 1) * 4] = hgrn_lb_raw[l].reshape(4, 128).T
        c[:, 48 + l] = hgrn_norm_w[l]
        c[0:8, 50 + l] = fox_f_bias[l]
    c[:, 32:40] = norm_final_w.reshape(8, 128).T
    return c


def kernel(x, meta, norm_mix_w, w_in, fox_f_bias, hgrn_lb_raw, hgrn_norm_w, w_out,
           norm_ffn_w, w_ffn_gate, w_ffn_up, w_ffn_down, norm_final_w, _debug=False):
    f = lambda a: np.ascontiguousarray(np.asarray(a, dtype=np.float32))
    x, meta = f(x), f(meta)
    consts = _prep_consts(f(norm_mix_w), f(norm_ffn_w), f(norm_final_w), f(hgrn_lb_raw), f(hgrn_norm_w), f(fox_f_bias))
    key = bool(_debug)
    if key not in _NC_CACHE:
        _NC_CACHE[key] = build_program(debug=_debug)
    nc = _NC_CACHE[key]
    B = x.shape[0]
    in_maps = []
    for core in range(8):
        b = core % B
        xin = np.zeros((LP, D), np.float32)
        xin[0:NMETA] = meta
        xin[NMETA:NMETA + SEQ] = x[b]
        in_maps.append({"xin": xin, "w_in": f(w_in), "w_out": f(w_out), "w_g": f(w_ffn_gate), "w_u": f(w_ffn_up),
                        "w_d": f(w_ffn_down), "consts": consts})
    res = run_bass_kernel_spmd(nc, in_maps, core_ids=list(range(8)))
    if _debug:
        return res
    return np.stack([np.asarray(res.results[b]["out"], dtype=np.float32) for b in range(B)], axis=0)
```

```python
import contextlib
import os
import numpy as np
import concourse.bass as bass
import concourse.mybir as mybir
from concourse.bass_utils import run_bass_kernel_spmd

F32 = mybir.dt.float32
BF16 = mybir.dt.bfloat16
ALU = mybir.AluOpType
AF = mybir.ActivationFunctionType

D = 1024
SEQ = 4096
NMETA = 16
LP = 4224
NT = LP // 128
NCH = LP // 64
INC = 3592
DFF = 2816
NF = DFF // 128
EPS = 1e-6
SUPER = [(i * 512, 512) for i in range(8)] + [(4096, 128)]
SUPER_E = [(i * 256, 256) for i in range(16)] + [(4096, 128)]
ARENA_W = 45000
CARENA_W = 3600
NCONST = 52

ENGS = ("pe", "act", "dve", "pool", "sp")


class Buf:
    __slots__ = ("name", "writer", "readers", "dma_cnt", "dma_id", "dma_ops")

    def __init__(self, name=""):
        self.name = name
        self.writer = None
        self.readers = []
        self.dma_cnt = 0
        self.dma_id = None
        self.dma_ops = []


class Op:
    __slots__ = ("eng", "fn", "waits", "flag", "val", "dma_buf", "is_load", "sp_idx", "dsp")

    def __init__(self, eng, fn):
        self.eng = eng
        self.fn = fn
        self.waits = []
        self.flag = False
        self.val = None
        self.dma_buf = None
        self.is_load = False
        self.sp_idx = -1
        self.dsp = -1


class Sched:
    def __init__(self, nc):
        self.nc = nc
        self.streams = {e: [] for e in ENGS}
        self.dma_bufs = []
        self.all_ops = []

    def _deps(self, op, reads, writes):
        toks = []
        for b in reads:
            if b.writer is not None:
                toks.append(b.writer)
        for b in writes:
            if b.writer is not None:
                toks.append(b.writer)
            toks.extend(b.readers)
        for t in toks:
            if t[0] == "op":
                p = t[1]
                if p is op:
                    continue
                if p.eng == op.eng and op.eng in ("pe", "sp"):
                    continue
                p.flag = True
            else:
                t = ("dma", t[1], t[1].dma_cnt)
            op.waits.append(t)

    def op(self, eng, fn, reads=(), writes=()):
        o = Op(eng, fn)
        self._deps(o, reads, writes)
        tok = ("op", o)
        for b in reads:
            b.readers = [t for t in b.readers if not (t[0] == "op" and t[1].eng == eng)]
            b.readers.append(tok)
        for b in writes:
            b.writer = tok
            b.readers = []
        self.streams[eng].append(o)
        self.all_ops.append(o)
        return o

    def dma(self, eng, out_ap, in_ap, sbuf, reads=(), writes=()):
        eng = "sp"

        def fn(e):
            return e.dma_start(out=out_ap, in_=in_ap)
        o = Op(eng, fn)
        o.is_load = len(writes) > 0
        self._deps(o, reads, writes)
        if sbuf.dma_id is None:
            sbuf.dma_id = len(self.dma_bufs)
            self.dma_bufs.append(sbuf)
        sbuf.dma_cnt += 1
        sbuf.dma_ops.append(o)
        o.dma_buf = sbuf
        tok = ("dma", sbuf, sbuf.dma_cnt)
        for b in reads:
            b.readers.append(tok)
        for b in writes:
            b.writer = tok
            b.readers = []
        self.streams[eng].append(o)
        self.all_ops.append(o)
        return o

    def _reorder_sp(self):
        sp = self.streams["sp"]
        for i, o in enumerate(sp):
            o.sp_idx = i
        last = {e: -1 for e in ENGS}
        for o in self.all_ops:
            v = last[o.eng] if o.eng != "sp" else -1
            for t in o.waits:
                if t[0] == "op":
                    v = max(v, t[1].dsp)
                else:
                    for q in t[1].dma_ops[:t[2]][-3:]:
                        v = max(v, q.sp_idx, q.dsp)
                    if t[2] > 3:
                        v = max(v, t[1].dma_ops[t[2] - 4].sp_idx)
            o.dsp = v
            if o.eng != "sp":
                last[o.eng] = max(last[o.eng], v)
        keyed = []
        prev_load_key = -1.0
        floor = -1.0
        for i, o in enumerate(sp):
            if o.fn is None:
                floor = float(i)
            if o.dma_buf is not None and o.is_load:
                k = max(float(o.dsp) + 0.5, prev_load_key, floor + 0.5)
                k = min(k, float(i))
                prev_load_key = k
                keyed.append((k, i, o))
            else:
                keyed.append((float(i), i, o))
        keyed.sort(key=lambda x: (x[0], x[1]))
        self.streams["sp"] = [x[2] for x in keyed]

    def barrier(self):
        lasts = {}
        for e in ENGS:
            for o in reversed(self.streams[e]):
                if o.dma_buf is None and o.fn is not None:
                    lasts[e] = o
                    break
        for e in ENGS:
            o = Op(e, None)
            for e2, p in lasts.items():
                if e2 != e:
                    p.flag = True
                    o.waits.append(("op", p))
            for b in self.dma_bufs:
                if b.dma_cnt:
                    o.waits.append(("dma", b, b.dma_cnt))
            self.streams[e].append(o)
            self.all_ops.append(o)

    def emit(self):
        nc = self.nc
        self._reorder_sp()
        with contextlib.ExitStack() as es:
            esem = {e: es.enter_context(nc.semaphore("s_" + e)) for e in ENGS}
            dsem = [es.enter_context(nc.semaphore("d%d" % i)) for i in range(len(self.dma_bufs))]
            for e in ENGS:
                c = 0
                for o in self.streams[e]:
                    if o.flag:
                        c += 1
                        o.val = c
            block = es.enter_context(nc.Block())

            def run(e, eng):
                seen = {}
                for o in self.streams[e]:
                    need = {}
                    for t in o.waits:
                        if t[0] == "op":
                            key, val, sem = ("e", t[1].eng), t[1].val, esem[t[1].eng]
                        else:
                            key, val, sem = ("d", t[1].dma_id), 16 * t[2], dsem[t[1].dma_id]
                        if key not in need or need[key][0] < val:
                            need[key] = (val, sem)
                    for key, (val, sem) in need.items():
                        if seen.get(key, 0) >= val:
                            continue
                        seen[key] = val
                        eng.wait_ge(sem, val)
                    if o.fn is None:
                        continue
                    ins = o.fn(eng)
                    if o.dma_buf is not None:
                        ins.then_inc(dsem[o.dma_buf.dma_id], 16)
                    elif o.flag:
                        ins.then_inc(esem[e], 1)

            @block.tensor
            def _(eng):
                run("pe", eng)

            @block.scalar
            def _(eng):
                run("act", eng)

            @block.vector
            def _(eng):
                run("dve", eng)

            @block.gpsimd
            def _(eng):
                run("pool", eng)

            @block.sync
            def _(eng):
                run("sp", eng)


class Arena:
    def __init__(self, ap_f32, width):
        self.ap = ap_f32
        self.width = width
        self.limit = width
        self.off = 0

    def reset(self, off=0):
        self.off = off

    def alloc(self, nelem, dtype=F32):
        w = (nelem + 1) // 2 if dtype == BF16 else nelem
        w = (w + 7) // 8 * 8
        assert self.off + w <= self.limit, ("arena overflow", self.off, w, self.limit)
        a = self.ap[:, self.off:self.off + w]
        self.off += w
        if dtype == BF16:
            a = a.bitcast(BF16)
        return a[:, 0:nelem]


def build_program(debug=False, upto=None):
    nc = bass.Bass("TRN2", target_bir_lowering=False)
    dt = nc.dram_tensor
    xin = dt("xin", [LP, D], F32, kind="ExternalInput").ap()
    w_in = dt("w_in", [2, D, INC], F32, kind="ExternalInput").ap()
    w_out = dt("w_out", [2, D, D], F32, kind="ExternalInput").ap()
    w_g = dt("w_g", [2, D, DFF], F32, kind="ExternalInput").ap()
    w_u = dt("w_u", [2, D, DFF], F32, kind="ExternalInput").ap()
    w_d = dt("w_d", [2, DFF, D], F32, kind="ExternalInput").ap()
    cst_d = dt("consts", [128, NCONST], F32, kind="ExternalInput").ap()
    out = dt("out", [SEQ, D], F32, kind="ExternalOutput").ap()
    kd = "ExternalOutput" if debug else "Internal"
    hT = dt("hT", [D, LP], F32, kind=kd).ap()
    Qd = dt("Qd", [8, 70, LP], BF16, kind=kd).ap()
    Kd = dt("Kd", [8, 70, LP], BF16, kind=kd).ap()
    Vd = dt("Vd", [8, 128, NT * 65], BF16, kind=kd).ap()
    QEd = dt("QEd", [4, 128, LP], BF16, kind=kd).ap()
    KEd = dt("KEd", [4, 128, LP], BF16, kind=kd).ap()
    Gd = dt("Gd", [4, 128, LP], BF16, kind=kd).ap()
    ELd = dt("ELd", [4, 128, NCH], F32, kind=kd).ap()
    VHd = dt("VHd", [NCH, 64, 512], BF16, kind=kd).ap()
    Cfd = dt("Cfd", [4, 128, LP], BF16, kind=kd).ap()
    Chd = dt("Chd", [4, 128, LP], BF16, kind=kd).ap()
    hT_v = hT.rearrange("(c p) t -> p c t", p=128)

    es = contextlib.ExitStack()
    with es:
        arena_t = es.enter_context(nc.sbuf_tensor("arena", [128, ARENA_W], F32))
        carena_t = es.enter_context(nc.sbuf_tensor("carena", [128, CARENA_W], F32))
        banks = [es.enter_context(nc.psum_tensor("bank%d" % i, [128, 512], F32)) for i in range(8)]
        pa = [b.ap() for b in banks]
        pab = [b.ap().bitcast(BF16) for b in banks]
        pb = [Buf("pb%d" % i) for i in range(8)]
        A = Arena(arena_t.ap(), ARENA_W)
        WF_W = 2 * (8 * DFF // 2)
        WF_OFF = ARENA_W - WF_W
        ffn_w = {}
        CA = Arena(carena_t.ap(), CARENA_W)
        S = Sched(nc)
        bankctr = [0]

        def nextbank(lo=0, hi=8):
            b = lo + bankctr[0] % (hi - lo)
            bankctr[0] += 1
            return b

        def MM(o, lhsT, rhs, start, stop, reads, writes):
            S.op("pe", lambda e: e.matmul(o, lhsT=lhsT, rhs=rhs, start=start, stop=stop), reads, writes)

        def TR(o, in_, ident, reads, writes):
            S.op("pe", lambda e: e.transpose(out=o, in_=in_, identity=ident), reads, writes)

        def ACTV(o, in_, func, reads, writes, bias=0.0, scale=1.0):
            S.op("act", lambda e: e.activation(out=o, in_=in_, func=func, bias=bias, scale=scale), reads, writes)

        def COPY(eng, o, in_, reads, writes):
            if eng == "act":
                S.op("act", lambda e: e.copy(out=o, in_=in_), reads, writes)
            else:
                S.op(eng, lambda e: e.tensor_copy(out=o, in_=in_), reads, writes)

        def SCALE(eng, o, in_, sc, reads, writes):
            if eng == "act":
                S.op("act", lambda e: e.activation(out=o, in_=in_, func=AF.Copy, scale=sc), reads, writes)
            else:
                S.op(eng, lambda e: e.tensor_scalar(out=o, in0=in_, scalar1=sc, scalar2=None, op0=ALU.mult), reads, writes)

        def TT(eng, o, in0, in1, op, reads, writes):
            S.op(eng, lambda e: e.tensor_tensor(out=o, in0=in0, in1=in1, op=op), reads, writes)

        def TS(eng, o, in0, s1, s2, op0, op1, reads, writes):
            if s2 is None:
                S.op(eng, lambda e: e.tensor_scalar(out=o, in0=in0, scalar1=s1, scalar2=None, op0=op0), reads, writes)
            else:
                S.op(eng, lambda e: e.tensor_scalar(out=o, in0=in0, scalar1=s1, scalar2=s2, op0=op0, op1=op1), reads, writes)

        def STT(eng, o, in0, scalar, in1, op0, op1, reads, writes):
            S.op(eng, lambda e: e.scalar_tensor_tensor(out=o, in0=in0, scalar=scalar, in1=in1, op0=op0, op1=op1), reads, writes)

        def MEMSET(eng, o, val, writes):
            S.op(eng, lambda e: e.memset(o, val), (), writes)

        def RECIP(o, in_, reads, writes):
            S.op("dve", lambda e: e.reciprocal(out=o, in_=in_), reads, writes)

        bC = Buf("consts")
        ident32 = CA.alloc(128)
        identb = CA.alloc(128, BF16)
        onesb = CA.alloc(128, BF16)
        onesf = CA.alloc(128)
        maskA = CA.alloc(64)
        cst = CA.alloc(NCONST)
        oml = CA.alloc(8)
        negb = CA.alloc(2)
        ones8 = CA.alloc(512)
        segm = CA.alloc(512)
        ones3 = CA.alloc(3 * 512, BF16)
        tmpc = CA.alloc(4)
        ELp = CA.alloc(3 * 4 * NCH)
        ELp4 = ELp.rearrange("p (k h c) -> p k h c", h=4, c=NCH)
        bEL = Buf("EL")
        MEMSET("pool", ident32, 0.0, [bC])
        S.op("pool", lambda e: e.affine_select(out=ident32, in_=ident32, pattern=[[1, 128]], compare_op=ALU.not_equal,
                                              fill=1.0, base=0, channel_multiplier=-1), [bC], [bC])
        COPY("pool", identb, ident32, [bC], [bC])
        MEMSET("pool", onesf, 1.0, [bC])
        COPY("pool", onesb, onesf, [bC], [bC])
        S.op("pool", lambda e: e.affine_select(out=maskA[0:64, :], in_=onesf[0:64, 0:64], pattern=[[1, 64]],
                                              compare_op=ALU.is_ge, fill=0.0, base=0, channel_multiplier=-1), [bC], [bC])
        S.dma("sp", cst, cst_d, bC, writes=[bC])
        MEMSET("pool", oml, 1.0, [bC])
        TT("dve", tmpc, cst[:, 40:44], cst[:, 44:48], ALU.subtract, [bC], [bC])
        ACTV(oml[:, 4:8], tmpc, AF.Sigmoid, [bC], [bC])
        TS("dve", negb[0:8, :], cst[0:8, 50:52], -1.0, None, ALU.mult, None, [bC], [bC])
        omlh = CA.alloc(8)
        nomlh = CA.alloc(8)
        TS("dve", omlh, oml, 0.5, None, ALU.mult, None, [bC], [bC])
        TS("dve", nomlh, oml, -0.5, None, ALU.mult, None, [bC], [bC])
        MEMSET("pool", ones8, 1.0, [bC])
        MEMSET("pool", segm, 1.0, [bC])
        MEMSET("pool", segm.rearrange("p (c t) -> p c t", t=64)[:, :, 0:1], 0.0, [bC])
        MEMSET("pool", ones3, 1.0, [bC])

        def rms_feat(hs3, b_hs, n, sq3, b_sq, rs, b_rs, nfeat):
            nchunk = hs3.shape[1]
            ACTV(sq3[:, :, 0:n], hs3[:, :, 0:n], AF.Square, [b_hs], [b_sq])
            bk = nextbank()
            for c in range(nchunk):
                MM(pa[bk][:, 0:n], onesb, sq3[:, c, 0:n], c == 0, c == nchunk - 1, [b_sq, bC], [pb[bk]])
            TS("dve", rs[:, 0:n], pa[bk][:, 0:n], 1.0 / nfeat, EPS, ALU.mult, ALU.add, [pb[bk]], [b_rs])
            ACTV(rs[:, 0:n], rs[:, 0:n], AF.Ln, [b_rs], [b_rs])
            ACTV(rs[:, 0:n], rs[:, 0:n], AF.Exp, [b_rs], [b_rs], scale=-0.5)

        def pass0():
            A.reset()
            A.limit = ARENA_W
            xt = [A.alloc(D) for _ in range(2)]
            bxt = [Buf(), Buf()]
            ht = [A.alloc(8 * 512) for _ in range(2)]
            bht = [Buf(), Buf()]
            k = 0
            for si, (t0, n) in enumerate(SUPER):
                h3 = ht[si % 2].rearrange("p (c t) -> p c t", t=512)
                bh = bht[si % 2]
                for jj in range(n // 128):
                    j = t0 // 128 + jj
                    x_, bx = xt[k % 2], bxt[k % 2]
                    k += 1
                    S.dma("sp", x_, xin[j * 128:(j + 1) * 128, :], bx, writes=[bx])
                    for half in range(2):
                        bk = nextbank()
                        for q in range(4):
                            c = half * 4 + q
                            TR(pa[bk][:, q * 128:(q + 1) * 128], x_[:, c * 128:(c + 1) * 128], ident32,
                               [bx, bC], [pb[bk]])
                        COPY("act" if half == 0 else "dve", h3[:, half * 4:half * 4 + 4, jj * 128:(jj + 1) * 128],
                             pa[bk].rearrange("p (q t) -> p q t", t=128), [pb[bk]], [bh])
                S.dma("pool", hT_v[:, :, t0:t0 + n], h3[:, :, 0:n], bh, reads=[bh])
            S.barrier()

        def passA(l):
            A.reset()
            A.limit = ARENA_W
            Win = A.alloc(8 * INC, BF16)
            Win3 = Win.rearrange("p (c n) -> p c n", n=INC)
            bW = [Buf() for _ in range(8)]
            mark = A.off
            stg = [A.alloc(INC) for _ in range(2)]
            bstg = [Buf(), Buf()]
            for c in range(8):
                s_, bs = stg[c % 2], bstg[c % 2]
                S.dma("sp", s_, w_in[l, c * 128:(c + 1) * 128, :], bs, writes=[bs])
                half = INC // 2
                SCALE("act", Win3[:, c, 0:half], s_[:, 0:half], cst[:, l * 8 + c:l * 8 + c + 1], [bs, bC], [bW[c]])
                TS("dve", Win3[:, c, half:INC], s_[:, half:INC], cst[:, l * 8 + c:l * 8 + c + 1], None, ALU.mult, None,
                   [bs, bC], [bW[c]])
            S.barrier()
            A.reset(mark)
            hs = A.alloc(8 * 512); b_hs = Buf()
            hs3 = hs.rearrange("p (c t) -> p c t", t=512)
            sq = A.alloc(8 * 512, BF16); b_sq = Buf()
            sq3 = sq.rearrange("p (c t) -> p c t", t=512)
            rs = A.alloc(512); b_rs = Buf()
            uT = [A.alloc(8 * 512, BF16) for _ in range(2)]
            b_uT = [Buf(), Buf()]
            Qst = A.alloc(4 * 512, BF16); b_Q = Buf()
            Kst = A.alloc(4 * 512, BF16); b_K = Buf()
            Q3 = Qst.rearrange("p (h t) -> p h t", t=512)
            K3 = Kst.rearrange("p (h t) -> p h t", t=512)
            Vst = A.alloc(4 * 8 * 65, BF16); b_V = Buf()
            V4 = Vst.rearrange("p (h j d) -> p h j d", j=4, d=65)
            V3m = Vst.rearrange("p (h m) -> p h m", m=4 * 65)
            MEMSET("pool", Vst, 1.0, [b_V])
            t1 = A.alloc(512); t2 = A.alloc(512); cc = A.alloc(512); r1 = A.alloc(512)
            b_f = Buf()
            cs3 = A.alloc(6 * 512, BF16); b_cs = Buf()
            cs3v = cs3.rearrange("p (r t) -> p r t", t=512)
            carry = A.alloc(1); b_carry = Buf()
            MEMSET("dve", carry, 0.0, [b_carry])
            kk = A.alloc(512); lf = A.alloc(512); bb = A.alloc(512)
            eb = A.alloc(512); enb = A.alloc(512)
            th = [A.alloc(512) for _ in range(4)]; b_th = [Buf() for _ in range(4)]
            sqhs = [A.alloc(512) for _ in range(4)]; b_sqhs = [Buf() for _ in range(4)]
            b_h = [Buf() for _ in range(7)]
            qe = A.alloc(4 * 512, BF16); ke = A.alloc(4 * 512, BF16); gg = A.alloc(4 * 512, BF16)
            b_qe, b_ke, b_gg = Buf(), Buf(), Buf()
            qe3 = qe.rearrange("p (h t) -> p h t", t=512)
            ke3 = ke.rearrange("p (h t) -> p h t", t=512)
            gg3 = gg.rearrange("p (h t) -> p h t", t=512)
            VH = A.alloc(4 * 512, BF16); b_VH = Buf()
            VH3 = VH.rearrange("p (c n) -> p c n", n=512)
            allW = bW

            def norm(si):
                t0, n = SUPER[si]
                u3n = uT[si % 2].rearrange("p (c t) -> p c t", t=512)
                S.dma("sp", hs3[:, :, 0:n], hT_v[:, :, t0:t0 + n], b_hs, writes=[b_hs])
                rms_feat(hs3, b_hs, n, sq3, b_sq, rs, b_rs, D)
                for c in range(8):
                    TT("dve", u3n[:, c, 0:n], hs3[:, c, 0:n], rs[:, 0:n], ALU.mult,
                       [b_hs, b_rs], [b_uT[si % 2]])

            norm(0)
            ypend = []
            for si, (t0, n) in enumerate(SUPER):
                u_, bu = uT[si % 2], b_uT[si % 2]
                u3 = u_.rearrange("p (c t) -> p c t", t=512)
                rd = [bu] + allW

                def proj_fm(c0, m, bk):
                    for c in range(8):
                        MM(pa[bk][0:m, 0:n], Win3[:, c, c0:c0 + m], u3[:, c, 0:n], c == 0, c == 7, rd, [pb[bk]])

                bk = nextbank()
                proj_fm(1536, 8, bk)
                ACTV(t1[0:8, 0:n], pa[bk][0:8, 0:n], AF.Exp, [pb[bk], bC], [b_f], bias=negb[0:8, l:l + 1], scale=-1.0)
                ACTV(t2[0:8, 0:n], t1[0:8, 0:n], AF.Ln, [b_f], [b_f], bias=1.0, scale=1.0)
                S.op("dve", lambda e, n=n: e.tensor_tensor_scan(out=cc[0:8, 0:n], data0=ones8[0:8, 0:n], data1=t2[0:8, 0:n],
                                                             initial=carry[0:8, 0:1], op0=ALU.mult, op1=ALU.subtract),
                     [b_f, b_carry, bC], [b_f])
                COPY("dve", carry[0:8, 0:1], cc[0:8, n - 1:n], [b_f], [b_carry])
                COPY("dve", cs3v[0:8, 0, 0:n], cc[0:8, 0:n], [b_f], [b_cs])
                TT("dve", r1[0:8, 0:n], cc[0:8, 0:n], cs3v[0:8, 0, 0:n], ALU.subtract, [b_f, b_cs], [b_f])
                COPY("dve", cs3v[0:8, 1, 0:n], r1[0:8, 0:n], [b_f], [b_cs])
                TT("dve", r1[0:8, 0:n], r1[0:8, 0:n], cs3v[0:8, 1, 0:n], ALU.subtract, [b_f, b_cs], [b_f])
                COPY("dve", cs3v[0:8, 2, 0:n], r1[0:8, 0:n], [b_f], [b_cs])
                TS("dve", cs3v[0:8, 3:6, 0:n], cs3v[0:8, 0:3, 0:n], -1.0, None, ALU.mult, None, [b_cs], [b_cs])
                S.dma("pool", Qd[:, 64:67, t0:t0 + n], cs3v[0:8, 0:3, 0:n], b_cs, reads=[b_cs])
                S.dma("pool", Kd[:, 67:70, t0:t0 + n], cs3v[0:8, 3:6, 0:n], b_cs, reads=[b_cs])
                o3 = ones3.rearrange("p (r t) -> p r t", t=512)
                S.dma("pool", Qd[:, 67:70, t0:t0 + n], o3[0:8, :, 0:n], bC, reads=[bC])
                S.dma("pool", Kd[:, 64:67, t0:t0 + n], o3[0:8, :, 0:n], bC, reads=[bC])
                if ypend:
                    ypend.pop(0)()
                Qd_v = Qd.rearrange("(m two) p t -> two p m t", two=2)
                Kd_v = Kd.rearrange("(m two) p t -> two p m t", two=2)
                for m in range(4):
                    bk = nextbank()
                    proj_fm(m * 128, 128, bk)
                    ACTV(Q3[:, m, 0:n], pa[bk][:, 0:n], AF.Copy, [pb[bk]], [b_Q], scale=0.125)
                for two in range(2):
                    S.dma("pool", Qd_v[two, 0:64, :, t0:t0 + n], Q3[two * 64:(two + 1) * 64, 0:4, 0:n], b_Q, reads=[b_Q])
                if ypend:
                    ypend.pop(0)()
                for m in range(4):
                    bk = nextbank()
                    proj_fm(512 + m * 128, 128, bk)
                    COPY("dve", K3[:, m, 0:n], pa[bk][:, 0:n], [pb[bk]], [b_K])
                for two in range(2):
                    S.dma("pool", Kd_v[two, 0:64, :, t0:t0 + n], K3[two * 64:(two + 1) * 64, 0:4, 0:n], b_K, reads=[b_K])
                if ypend:
                    ypend.pop(0)()
                for jj in range(n // 128):
                    bk = nextbank()
                    for c in range(8):
                        MM(pa[bk][:, 0:512], u3[:, c, jj * 128:(jj + 1) * 128], Win3[:, c, 1024:1536], c == 0, c == 7,
                           rd, [pb[bk]])
                    COPY("act" if jj % 2 == 0 else "dve", V4[:, :, jj, 0:64],
                         pa[bk].rearrange("p (h d) -> p h d", d=64), [pb[bk]], [b_V])
                j0 = t0 // 128
                nj = n // 128
                S.dma("pool", Vd[:, :, j0 * 65:(j0 + nj) * 65].rearrange("h p m -> p h m"),
                      V3m[:, :, 0:nj * 65], b_V, reads=[b_V])
                while ypend:
                    ypend.pop(0)()
                for h in range(4):
                    bk = nextbank()
                    proj_fm(1544 + 512 + h * 128, 128, bk)
                    ACTV(th[h][:, 0:n], pa[bk][:, 0:n], AF.Tanh, [pb[bk]], [b_th[h]], scale=0.5)
                    bk = nextbank()
                    proj_fm(1544 + h * 128, 128, bk)
                    ACTV(sqhs[h][:, 0:n], pa[bk][:, 0:n], AF.Silu, [pb[bk]], [b_sqhs[h]])
                    bk = nextbank()
                    proj_fm(1544 + 1536 + h * 128, 128, bk)
                    ACTV(gg3[:, h, 0:n], pa[bk][:, 0:n], AF.Silu, [pb[bk]], [b_gg])
                if si + 1 < len(SUPER):
                    norm(si + 1)
                def make_y(h, t0, n):
                    def ypiece():
                        TS("dve", kk[:, 0:n], th[h][:, 0:n], nomlh[:, l * 4 + h:l * 4 + h + 1],
                           omlh[:, l * 4 + h:l * 4 + h + 1], ALU.mult, ALU.add, [b_th[h], bC], [b_h[1]])
                        ACTV(lf[:, 0:n], kk[:, 0:n], AF.Ln, [b_h[1]], [b_h[2]], bias=1.0, scale=-1.0)
                        TS("dve", lf[:, 0:n], lf[:, 0:n], -30.0, None, ALU.max, None, [b_h[2]], [b_h[2]])
                        S.op("dve", lambda e: e.tensor_tensor_scan(out=bb[:, 0:n], data0=segm[:, 0:n], data1=lf[:, 0:n],
                                                                  initial=0.0, op0=ALU.mult, op1=ALU.add),
                             [b_h[2], bC], [b_h[3]])
                        c0_, c1_ = t0 // 64, (t0 + n) // 64
                        bb3 = bb[:, 0:n].rearrange("p (c t) -> p c t", t=64)
                        lf3 = lf[:, 0:n].rearrange("p (c t) -> p c t", t=64)
                        ACTV(ELp4[:, 0, h, c0_:c1_], bb3[:, :, 63], AF.Exp, [b_h[3]], [bEL])
                        ACTV(ELp4[:, 2, h, c0_:c1_], bb3[:, :, 31], AF.Exp, [b_h[3]], [bEL])
                        TT("dve", lf3, bb3, bb3[:, :, 31:32].broadcast_to([128, n // 64, 64]), ALU.subtract,
                           [b_h[3], b_h[2]], [b_h[2]])
                        ACTV(eb[:, 0:n], lf[:, 0:n], AF.Exp, [b_h[2]], [b_h[4]])
                        ACTV(enb[:, 0:n], lf[:, 0:n], AF.Exp, [b_h[2]], [b_h[5]], scale=-1.0)
                        TT("dve", ke3[:, h, 0:n], kk[:, 0:n], enb[:, 0:n], ALU.mult, [b_h[1], b_h[5]], [b_ke])
                        COPY("pool", ELp4[:, 1, h, c0_:c1_], eb[:, 0:n].rearrange("p (c t) -> p c t", t=64)[:, :, 63],
                             [b_h[4]], [bEL])
                        STT("dve", qe3[:, h, 0:n], sqhs[h][:, 0:n], 128.0 ** -0.5, eb[:, 0:n], ALU.mult, ALU.mult,
                            [b_sqhs[h], b_h[4]], [b_qe])
                    return ypiece

                def make_ystores(t0, n):
                    def ystores():
                        S.dma("pool", QEd[:, :, t0:t0 + n].rearrange("h p t -> p h t"), qe3[:, :, 0:n], b_qe, reads=[b_qe])
                        S.dma("pool", KEd[:, :, t0:t0 + n].rearrange("h p t -> p h t"), ke3[:, :, 0:n], b_ke, reads=[b_ke])
                    return ystores
                ypend.extend([make_y(h, t0, n) for h in range(4)] + [make_ystores(t0, n)])
                S.dma("pool", Gd[:, :, t0:t0 + n].rearrange("h p t -> p h t"), gg3[:, :, 0:n], b_gg, reads=[b_gg])
                VHd_tok = VHd.rearrange("c p n -> (c p) n")
                for jj in range(n // 128):
                    bk = nextbank()
                    for c in range(8):
                        MM(pa[bk][:, 0:512], u3[:, c, jj * 128:(jj + 1) * 128], Win3[:, c, 1544 + 1024:1544 + 1536],
                           c == 0, c == 7, rd, [pb[bk]])
                    COPY("act" if jj % 2 == 0 else "dve", VH3[:, jj, :], pa[bk][:, 0:512], [pb[bk]], [b_VH])
                S.dma("pool", VHd_tok[t0:t0 + n, :].rearrange("(j p) n -> p j n", p=128), VH3[:, 0:n // 128, :],
                      b_VH, reads=[b_VH])
            while ypend:
                ypend.pop(0)()
            S.barrier()

        def passB(l):
            A.reset()
            A.limit = WF_OFF
            AW = Arena(arena_t.ap()[:, WF_OFF:ARENA_W], WF_W)
            Wg3 = AW.alloc(8 * DFF, BF16).rearrange("p (c n) -> p c n", n=DFF)
            Wu3 = AW.alloc(8 * DFF, BF16).rearrange("p (c n) -> p c n", n=DFF)
            bWg = [Buf() for _ in range(8)]; bWu = [Buf() for _ in range(8)]
            ffn_w[l] = (Wg3, Wu3, bWg, bWu)
            PW = DFF // 4
            wstg = [A.alloc(PW) for _ in range(2)]; bwstg = [Buf(), Buf()]
            pieces = []
            for (wsrc, W3, bWl) in ((w_g, Wg3, bWg), (w_u, Wu3, bWu)):
                for c in range(8):
                    for q in range(4):
                        pieces.append((wsrc[l, c * 128:(c + 1) * 128, q * PW:(q + 1) * PW], W3[:, c, q * PW:(q + 1) * PW],
                                       cst[:, 16 + l * 8 + c:16 + l * 8 + c + 1], bWl[c]))
            pstate = [0]

            def prefetch_step():
                k = pstate[0]
                if k >= len(pieces):
                    return
                pstate[0] += 1
                src, dst, sc, bw = pieces[k]
                S.dma("sp", wstg[k % 2], src, bwstg[k % 2], writes=[bwstg[k % 2]])
                SCALE("dve", dst, wstg[k % 2], sc, [bwstg[k % 2], bC], [bw])
            Ka = [A.alloc(LP, BF16) for _ in range(2)]; bKa = [Buf(), Buf()]
            Va = [A.alloc(NT * 65, BF16) for _ in range(2)]; bVa = [Buf(), Buf()]
            Qa = [A.alloc(512, BF16) for _ in range(2)]; bQa = [Buf(), Buf()]
            NPT = 7
            pt = [A.alloc(512, BF16) for _ in range(NPT)]; bpt = [Buf() for _ in range(NPT)]
            osb = [A.alloc(512) for _ in range(2)]; b_osb = [Buf(), Buf()]
            on = [A.alloc(512, BF16) for _ in range(2)]; b_on = [Buf(), Buf()]
            rcb = A.alloc(512); b_rcb = Buf()
            qi = 0
            pi = 0
            pending = [None]

            def make_part2(h, t0, n, par):
                def part2():
                    bk = 5
                    MM(pa[bk][0:64, 0:n], onesf[64:65, 0:64], osb[par][64:65, 0:n], True, True,
                       [b_osb[par], bC], [pb[bk]])
                    RECIP(rcb[0:64, 0:n], pa[bk][0:64, 0:n], [pb[bk]], [b_rcb])
                    TT("dve", on[par][0:64, 0:n], osb[par][0:64, 0:n], rcb[0:64, 0:n], ALU.mult,
                       [b_osb[par], b_rcb], [b_on[par]])
                    S.dma("pool", Cfd[h // 2, (h % 2) * 64:(h % 2) * 64 + 64, t0:t0 + n], on[par][0:64, 0:n], b_on[par], reads=[b_on[par]])
                return part2

            for h in range(8):
                ka, bka = Ka[h % 2], bKa[h % 2]
                va, bva = Va[h % 2], bVa[h % 2]
                va3 = va.rearrange("p (j d) -> p j d", d=65)
                S.dma("sp", ka[0:70, :], Kd[h], bka, writes=[bka])
                S.dma("sp", va, Vd[h], bva, writes=[bva])
                for si, (t0, n) in enumerate(SUPER):
                    qa, bqa = Qa[qi % 2], bQa[qi % 2]
                    par = qi % 2
                    ob = 6 + qi % 2
                    qi += 1
                    S.dma("sp", qa[0:70, 0:n], Qd[h, :, t0:t0 + n], bqa, writes=[bqa])
                    prefetch_step()
                    nk = (t0 + n) // 128
                    LA = 4
                    slots = {}
                    for idx in range(nk + LA):
                        if idx < nk:
                            j = idx
                            qlo = max(0, j * 128 - t0)
                            w = n - qlo
                            sb = nextbank(0, 5)
                            p_, bp = pt[pi % NPT], bpt[pi % NPT]
                            pi += 1
                            slots[j] = (p_, bp, qlo, w)
                            MM(pa[sb][:, 0:w], ka[0:70, j * 128:(j + 1) * 128], qa[0:70, qlo:n], True, True,
                               [bka, bqa], [pb[sb]])
                            ACTV(p_[:, 0:w], pa[sb][:, 0:w], AF.Exp, [pb[sb]], [bp])
                            if j * 128 >= t0:
                                S.op("pool", lambda e, p_=p_: e.affine_select(out=p_[:, 0:128], in_=p_[:, 0:128],
                                                                              pattern=[[1, 128]], compare_op=ALU.is_ge,
                                                                              fill=0.0, base=0, channel_multiplier=-1),
                                     [bp], [bp])
                        if idx >= LA:
                            j = idx - LA
                            p_, bp, qlo, w = slots.pop(j)
                            MM(pa[ob][0:65, qlo:n], va3[:, j, :], p_[:, 0:w], j == 0, j == nk - 1, [bva, bp], [pb[ob]])
                        if idx == min(LA + 2, nk + LA - 1) and pending[0] is not None:
                            pending[0]()
                            pending[0] = None
                    COPY("act", osb[par][0:65, 0:n], pa[ob][0:65, 0:n], [pb[ob]], [b_osb[par]])
                    pending[0] = make_part2(h, t0, n, par)
            pending[0]()
            while pstate[0] < len(pieces):
                prefetch_step()
            S.barrier()

        def passC(l):
            A.reset()
            A.limit = WF_OFF
            qeb = [A.alloc(4 * 512, BF16) for _ in range(2)]; b_qe = [Buf(), Buf()]
            keb = [A.alloc(4 * 512, BF16) for _ in range(2)]; b_ke = [Buf(), Buf()]
            ggb = [A.alloc(4 * 512, BF16) for _ in range(2)]; b_gg = [Buf(), Buf()]
            vhb = [A.alloc(8 * 512, BF16) for _ in range(2)]; b_vh = [Buf(), Buf()]
            Sf = [A.alloc(128) for _ in range(4)]; bSf = [Buf() for _ in range(4)]
            Sb = [A.alloc(128, BF16) for _ in range(4)]; bSb = [Buf() for _ in range(4)]
            Sel = [A.alloc(128) for _ in range(4)]; bSel = [Buf() for _ in range(4)]
            ktk = [A.alloc(128, BF16) for _ in range(4)]; bktk = [Buf() for _ in range(4)]
            at = [A.alloc(64, BF16) for _ in range(4)]; bat = [Buf() for _ in range(4)]
            osq = [A.alloc(512, BF16) for _ in range(2)]; bosq = [Buf(), Buf()]
            rs = [A.alloc(512) for _ in range(2)]; brs = [Buf(), Buf()]
            yy = [A.alloc(512) for _ in range(2)]; byy = [Buf(), Buf()]
            co = [A.alloc(4 * 512, BF16) for _ in range(2)]; bco = [Buf(), Buf()]
            for h in range(4):
                MEMSET("pool", Sf[h], 0.0, [bSf[h]])
                MEMSET("pool", Sb[h], 0.0, [bSb[h]])
                MEMSET("pool", at[h], 0.0, [bat[h]])
            k2 = 0
            for si, (t0, n) in enumerate(SUPER):
                r = si % 2
                qe3 = qeb[r].rearrange("p (h t) -> p h t", t=512)
                ke3 = keb[r].rearrange("p (h t) -> p h t", t=512)
                gg3 = ggb[r].rearrange("p (h t) -> p h t", t=512)
                vh3 = vhb[r].rearrange("p (c n) -> p c n", n=512)
                co3 = co[r].rearrange("p (h t) -> p h t", t=512)
                ncs = n // 64
                S.dma("sp", qe3[:, :, 0:n], QEd[:, :, t0:t0 + n].rearrange("h p t -> p h t"), b_qe[r], writes=[b_qe[r]])
                S.dma("sp", ke3[:, :, 0:n], KEd[:, :, t0:t0 + n].rearrange("h p t -> p h t"), b_ke[r], writes=[b_ke[r]])
                S.dma("sp", vh3[0:64, 0:ncs, :], VHd[t0 // 64:(t0 + n) // 64, :, :].rearrange("c p n -> p c n"),
                      b_vh[r], writes=[b_vh[r]])
                S.dma("sp", gg3[:, :, 0:n], Gd[:, :, t0:t0 + n].rearrange("h p t -> p h t"), b_gg[r], writes=[b_gg[r]])
                CDBG = ""
                for ci in range(ncs):
                    cs = slice(ci * 64, (ci + 1) * 64)
                    cg = t0 // 64 + ci
                    for h in range(4):
                        tb = 4 + h
                        kec = ke3[:, h, cs]
                        TR(pab[tb][0:64, 512:640], kec, identb, [b_ke[r], bC], [pb[tb]])
                        MM(pa[tb][0:32, 0:32], ke3[:, h, ci * 64:ci * 64 + 32], qe3[:, h, ci * 64:ci * 64 + 32],
                           True, True, [b_ke[r], b_qe[r]], [pb[tb]])
                        MM(pa[tb][0:64, 32:64], kec, qe3[:, h, ci * 64 + 32:ci * 64 + 64], True, True,
                           [b_ke[r], b_qe[r]], [pb[tb]])
                    for h in range(4):
                        tb = 4 + h
                        COPY("dve", ktk[h][0:64, :], pab[tb][0:64, 512:640], [pb[tb]], [bktk[h]])
                        TT("dve", at[h][0:32, 0:32], pa[tb][0:32, 0:32], maskA[0:32, 0:32], ALU.mult,
                           [pb[tb], bC], [bat[h]])
                        TT("dve", at[h][0:64, 32:64], pa[tb][0:64, 32:64], maskA[0:64, 32:64], ALU.mult,
                           [pb[tb], bC], [bat[h]])
                    for h in range(4):
                        SCALE("act", Sel[h], Sf[h], ELp4[:, 0, h, cg:cg + 1], [bSf[h], bEL], [bSel[h]])
                    for h in range(4):
                        tb = 4 + h
                        ob = h
                        qec = qe3[:, h, cs]
                        vhc = vh3[0:64, ci, h * 128:(h + 1) * 128]
                        MM(pa[ob][:, cs], Sb[h], qec, True, False, [bSb[h], b_qe[r]], [pb[ob]])
                        MM(pa[ob][:, cs], vhc, at[h][0:64, :], False, True, [b_vh[r], bat[h]], [pb[ob]])
                        MM(pa[tb][:, 128:256], ktk[h][0:64, :], vhc, True, True, [bktk[h], b_vh[r]], [pb[tb]])
                    for h in range(4):
                        tb = 4 + h
                        STT("dve", Sf[h], pa[tb][:, 128:256], ELp4[:, 1, h, cg:cg + 1], Sel[h], ALU.mult, ALU.add,
                            [pb[tb], bEL, bSel[h]], [bSf[h]])
                    if cg + 1 < NCH:
                        for h in range(4):
                            SCALE("act", Sb[h], Sf[h], ELp4[:, 2, h, cg + 1:cg + 2], [bSf[h], bEL], [bSb[h]])
                for h in range(4 if CDBG in ("", "post") else 0):
                    ob = h
                    tb = 4 + h
                    kk_ = k2 % 2
                    k2 += 1
                    ACTV(osq[kk_][:, 0:n], pa[ob][:, 0:n], AF.Square, [pb[ob]], [bosq[kk_]])
                    MM(pa[tb][:, 0:n], onesb, osq[kk_][:, 0:n], True, True, [bosq[kk_], bC], [pb[tb]])
                    TS("dve", rs[kk_][:, 0:n], pa[tb][:, 0:n], 1.0 / 128, EPS, ALU.mult, ALU.add, [pb[tb]], [brs[kk_]])
                    ACTV(rs[kk_][:, 0:n], rs[kk_][:, 0:n], AF.Ln, [brs[kk_]], [brs[kk_]])
                    ACTV(rs[kk_][:, 0:n], rs[kk_][:, 0:n], AF.Exp, [brs[kk_]], [brs[kk_]], scale=-0.5)
                    STT("dve", yy[kk_][:, 0:n], pa[ob][:, 0:n], cst[:, 48 + l:49 + l], rs[kk_][:, 0:n], ALU.mult, ALU.mult,
                        [pb[ob], brs[kk_], bC], [byy[kk_]])
                    TT("dve", co3[:, h, 0:n], yy[kk_][:, 0:n], gg3[:, h, 0:n], ALU.mult, [byy[kk_], b_gg[r]], [bco[r]])
                if CDBG in ("", "post"):
                    S.dma("pool", Chd[:, :, t0:t0 + n].rearrange("h p t -> p h t"), co3[:, :, 0:n], bco[r], reads=[bco[r]])
            S.barrier()

        def passD(l):
            A.reset()
            A.limit = WF_OFF
            Wo = A.alloc(8 * D, BF16); Wo3 = Wo.rearrange("p (h n) -> p h n", n=D); bWo = Buf()
            stg = A.alloc(8 * D); bstg = Buf()
            stg3 = stg.rearrange("p (h n) -> p h n", n=D)
            S.dma("sp", stg3, w_out[l].rearrange("(h p) n -> p h n", p=128), bstg, writes=[bstg])
            COPY("dve", Wo3[:, 0:4, :], stg3[:, 0:4, :], [bstg], [bWo])
            COPY("act", Wo3[:, 4:8, :], stg3[:, 4:8, :], [bstg], [bWo])
            S.barrier()
            A.reset(A.off - 8 * D)
            cf = [A.alloc(4 * 512, BF16) for _ in range(2)]; bcf = [Buf(), Buf()]
            ch = [A.alloc(4 * 512, BF16) for _ in range(2)]; bch = [Buf(), Buf()]
            hs = [A.alloc(8 * 512) for _ in range(2)]; bhs = [Buf(), Buf()]
            for si, (t0, n) in enumerate(SUPER):
                r = si % 2
                cf3 = cf[r].rearrange("p (h t) -> p h t", t=512)
                ch3 = ch[r].rearrange("p (h t) -> p h t", t=512)
                hs3 = hs[r].rearrange("p (c t) -> p c t", t=512)
                S.dma("sp", cf3[:, :, 0:n], Cfd[:, :, t0:t0 + n].rearrange("h p t -> p h t"), bcf[r], writes=[bcf[r]])
                S.dma("sp", ch3[:, :, 0:n], Chd[:, :, t0:t0 + n].rearrange("h p t -> p h t"), bch[r], writes=[bch[r]])
                S.dma("sp", hs3[:, :, 0:n], hT_v[:, :, t0:t0 + n], bhs[r], writes=[bhs[r]])
                for oc in range(8):
                    bk = nextbank()
                    osl = slice(oc * 128, (oc + 1) * 128)
                    for h in range(4):
                        MM(pa[bk][:, 0:n], Wo3[:, h, osl], cf3[:, h, 0:n], h == 0, False, [bWo, bcf[r]], [pb[bk]])
                    for h in range(4):
                        MM(pa[bk][:, 0:n], Wo3[:, 4 + h, osl], ch3[:, h, 0:n], False, h == 3, [bWo, bch[r]], [pb[bk]])
                    TT("dve", hs3[:, oc, 0:n], hs3[:, oc, 0:n], pa[bk][:, 0:n], ALU.add, [bhs[r], pb[bk]], [bhs[r]])
                S.dma("pool", hT_v[:, :, t0:t0 + n], hs3[:, :, 0:n], bhs[r], reads=[bhs[r]])
            S.barrier()

        def passE(l, last):
            A.reset()
            A.limit = WF_OFF
            Wg3, Wu3, bWg, bWu = ffn_w[l]
            Wd = A.alloc(NF * D, BF16); Wd3 = Wd.rearrange("p (f n) -> p f n", n=D)
            bWd = [Buf() for _ in range(NF)]
            stg = [A.alloc(512) for _ in range(2)]; bstg = [Buf(), Buf()]
            k = 0
            for f in range(NF):
                for q in range(2):
                    s_, bs = stg[k % 2], bstg[k % 2]
                    k += 1
                    S.dma("sp", s_[:, 0:512], w_d[l, f * 128:(f + 1) * 128, q * 512:(q + 1) * 512], bs, writes=[bs])
                    COPY("act" if k % 2 == 0 else "dve", Wd3[:, f, q * 512:(q + 1) * 512], s_[:, 0:512], [bs], [bWd[f]])
            NE = 256
            hsr = [A.alloc(8 * NE) for _ in range(2)]; bhsr = [Buf(), Buf()]
            uTr = [A.alloc(8 * NE, BF16) for _ in range(2)]; buTr = [Buf(), Buf()]
            aT = A.alloc(NF * NE, BF16); baT = Buf()
            a3 = aT.rearrange("p (f t) -> p f t", t=NE)
            sq3 = a3[:, 0:8, :]
            rs = A.alloc(NE); brs = Buf()
            sg = [A.alloc(NE) for _ in range(1)]; bsg = [Buf()]
            ot = [A.alloc(512 if last else 8) for _ in range(1)]; bot = [Buf()]
            allWg, allWu, allWd = bWg, bWu, bWd
            ko = 0
            kout = [0]
            oslots = [(ot[0], bot[0]), (stg[0], bstg[0]), (stg[1], bstg[1])]
            pending_out = [None]

            def norm(si):
                t0, n = SUPER_E[si]
                hs3n = hsr[si % 2].rearrange("p (c t) -> p c t", t=NE)
                u3n = uTr[si % 2].rearrange("p (c t) -> p c t", t=NE)
                S.dma("sp", hs3n[:, :, 0:n], hT_v[:, :, t0:t0 + n], bhsr[si % 2], writes=[bhsr[si % 2]])
                rms_feat(hs3n, bhsr[si % 2], n, u3n, buTr[si % 2], rs, brs, D)
                for c in range(8):
                    TT("dve", u3n[:, c, 0:n], hs3n[:, c, 0:n], rs[:, 0:n], ALU.mult,
                       [bhsr[si % 2], brs], [buTr[si % 2]])

            norm(0)
            for si, (t0, n) in enumerate(SUPER_E):
                hs3 = hsr[si % 2].rearrange("p (c t) -> p c t", t=NE)
                bhs = bhsr[si % 2]
                u3 = uTr[si % 2].rearrange("p (c t) -> p c t", t=NE)
                buT = buTr[si % 2]
                for f in range(NF):
                    if f == 3 and pending_out[0] is not None:
                        pending_out[0]()
                        pending_out[0] = None
                    fs = slice(f * 128, (f + 1) * 128)
                    bg = nextbank()
                    for c in range(8):
                        MM(pa[bg][:, 0:n], Wg3[:, c, fs], u3[:, c, 0:n], c == 0, c == 7, [buT] + allWg, [pb[bg]])
                    bu_ = nextbank()
                    for c in range(8):
                        MM(pa[bu_][:, 0:n], Wu3[:, c, fs], u3[:, c, 0:n], c == 0, c == 7, [buT] + allWu, [pb[bu_]])
                    s_, bs = sg[0], bsg[0]
                    ACTV(s_[:, 0:n], pa[bg][:, 0:n], AF.Silu, [pb[bg]], [bs])
                    TT("dve", a3[:, f, 0:n], s_[:, 0:n], pa[bu_][:, 0:n], ALU.mult, [bs, pb[bu_]], [baT])
                if si + 1 < len(SUPER_E):
                    norm(si + 1)
                for oc in range(8):
                    bk = nextbank()
                    osl = slice(oc * 128, (oc + 1) * 128)
                    for f in range(NF):
                        MM(pa[bk][:, 0:n], Wd3[:, f, osl], a3[:, f, 0:n], f == 0, f == NF - 1, [baT] + allWd, [pb[bk]])
                    TT("dve", hs3[:, oc, 0:n], hs3[:, oc, 0:n], pa[bk][:, 0:n], ALU.add, [bhs, pb[bk]], [bhs])
                if not last:
                    S.dma("pool", hT_v[:, :, t0:t0 + n], hs3[:, :, 0:n], bhs, reads=[bhs])
                else:
                    rms_feat(hs3, bhs, n, sq3, baT, rs, brs, D)
                    for c in range(8):
                        STT("dve", hs3[:, c, 0:n], hs3[:, c, 0:n], cst[:, 32 + c:33 + c],
                            rs[:, 0:n], ALU.mult, ALU.mult, [bhs, brs, bC], [bhs])
                    def make_out(hs3, bhs, t0, n):
                        def emit_out():
                            for jj in range(n // 128):
                                j = t0 // 128 + jj
                                g0 = j * 128
                                lo = max(g0, NMETA)
                                hi = min(g0 + 128, NMETA + SEQ)
                                for half in range(2):
                                    o_, bo = oslots[kout[0] % 3]
                                    kout[0] += 1
                                    bk = nextbank()
                                    for q in range(4):
                                        c = half * 4 + q
                                        TR(pa[bk][:, q * 128:(q + 1) * 128], hs3[:, c, jj * 128:(jj + 1) * 128], ident32,
                                           [bhs, bC], [pb[bk]])
                                    COPY("act" if half == 0 else "dve", o_[:, 0:512], pa[bk][:, 0:512], [pb[bk]], [bo])
                                    if hi > lo:
                                        S.dma("pool", out[lo - NMETA:hi - NMETA, half * 512:(half + 1) * 512],
                                              o_[lo - g0:hi - g0, 0:512], bo, reads=[bo])
                        return emit_out
                    pending_out[0] = make_out(hs3, bhs, t0, n)
            if pending_out[0] is not None:
                pending_out[0]()
            S.barrier()

        plan = [("0", pass0, ())]
        for l in range(2):
            plan += [("A%d" % l, passA, (l,)), ("B%d" % l, passB, (l,)), ("C%d" % l, passC, (l,)),
                     ("D%d" % l, passD, (l,)), ("E%d" % l, passE, (l, l == 1))]
        for name, fn_, args in plan:
            fn_(*args)
            if upto is not None and name == upto:
                break
        S.emit()
    return nc


_NC_CACHE = {}


def _prep_consts(norm_mix_w, norm_ffn_w, norm_final_w, hgrn_lb_raw, hgrn_norm_w, fox_f_bias):
    c = np.zeros((128, NCONST), np.float32)
    for l in range(2):
        c[:, l * 8:(l + 1) * 8] = norm_mix_w[l].reshape(8, 128).T
        c[:, 16 + l * 8:16 + (l + 1) * 8] = norm_ffn_w[l].reshape(8, 128).T
        c[:, 40 + l * 4:40 + (l + 1) * 4] = hgrn_lb_raw[l].reshape(4, 128).T
        c[:, 48 + l] = hgrn_norm_w[l]
        c[0:8, 50 + l] = fox_f_bias[l]
    c[:, 32:40] = norm_final_w.reshape(8, 128).T
    return c


def kernel(x, meta, norm_mix_w, w_in, fox_f_bias, hgrn_lb_raw, hgrn_norm_w, w_out,
           norm_ffn_w, w_ffn_gate, w_ffn_up, w_ffn_down, norm_final_w, _debug=False):
    f = lambda a: np.ascontiguousarray(np.asarray(a, dtype=np.float32))
    x, meta = f(x), f(meta)
    consts = _prep_consts(f(norm_mix_w), f(norm_ffn_w), f(norm_final_w), f(hgrn_lb_raw), f(hgrn_norm_w), f(fox_f_bias))
    key = bool(_debug)
    if key not in _NC_CACHE:
        _NC_CACHE[key] = build_program(debug=_debug)
    nc = _NC_CACHE[key]
    B = x.shape[0]
    in_maps = []
    for core in range(8):
        b = core % B
        xin = np.zeros((LP, D), np.float32)
        xin[0:NMETA] = meta
        xin[NMETA:NMETA + SEQ] = x[b]
        in_maps.append({"xin": xin, "w_in": f(w_in), "w_out": f(w_out), "w_g": f(w_ffn_gate), "w_u": f(w_ffn_up),
                        "w_d": f(w_ffn_down), "consts": consts})
    res = run_bass_kernel_spmd(nc, in_maps, core_ids=list(range(8)))
    if _debug:
        return res
    return np.stack([np.asarray(res.results[b]["out"], dtype=np.float32) for b in range(B)], axis=0)
```
